# Optimizing a Trainium2 kernel written in Bass

```python
import math
import jax, jax.numpy as jnp
from jax import lax
import numpy as np

D_MODEL = 1024
BATCH = 8
SEQ = 2048
DEPTH = 1

CHUNK = 64
D_CONV_A = D_MODEL
KERNEL_A = 31
D_INNER = 2 * D_MODEL
HEAD_DIM_SSM = 64
N_HEADS_SSM = D_INNER // HEAD_DIM_SSM
N_GROUPS = 4
D_STATE = 128
KERNEL_SSM = 4
D_XBC = D_INNER + 2 * N_GROUPS * D_STATE
D_IN_PROJ = 2 * D_CONV_A + D_INNER + D_XBC + N_HEADS_SSM
D_FF = 2816
KERNEL_FFN = 3
N_BRANCH = 2
N_MOD = 6
EPS = 1e-6
EPS_SSM_NORM = 1e-5

kernel_name = "hybrid_conformer_ssd_convffn_block"


def rmsnorm(x, g, eps=EPS):
    xf = x.astype(jnp.float32)
    y = xf * lax.rsqrt(jnp.mean(xf * xf, axis=-1, keepdims=True) + eps)
    return (y * g.astype(jnp.float32)).astype(x.dtype)


def layernorm(x, g, b, eps=EPS):
    xf = x.astype(jnp.float32)
    mu = jnp.mean(xf, axis=-1, keepdims=True)
    var = jnp.mean(jnp.square(xf - mu), axis=-1, keepdims=True)
    y = (xf - mu) * lax.rsqrt(var + eps)
    return (y * g.astype(jnp.float32) + b.astype(jnp.float32)).astype(x.dtype)


def causal_dwconv(x, w, b):
    k = w.shape[0]
    y = lax.conv_general_dilated(
        x, w[:, None, :].astype(x.dtype), window_strides=(1,), padding=[(k - 1, 0)],
        dimension_numbers=("NWC", "WIO", "NWC"), feature_group_count=x.shape[-1])
    return y + b


def modulate(h, shift, scale):
    return h * (1.0 + scale[:, None, :]) + shift[:, None, :]


def ssd_scan(xh, dt, a, bm, cm):
    bsz, seqlen, n_heads, hdim = xh.shape
    nc = seqlen // CHUNK
    hg = n_heads // N_GROUPS
    x = xh.astype(jnp.float32).reshape(bsz, nc, CHUNK, N_GROUPS, hg, hdim)
    dtc = dt.reshape(bsz, nc, CHUNK, N_GROUPS, hg)
    bc = bm.astype(jnp.float32).reshape(bsz, nc, CHUNK, N_GROUPS, D_STATE)
    cc = cm.astype(jnp.float32).reshape(bsz, nc, CHUNK, N_GROUPS, D_STATE)
    da = dtc * a.reshape(N_GROUPS, hg)
    acum = jnp.cumsum(da, axis=2)
    xdt = x * dtc[..., None]
    seg = acum[:, :, :, None] - acum[:, :, None, :]
    causal = jnp.tril(jnp.ones((CHUNK, CHUNK), dtype=bool))[None, None, :, :, None, None]
    decay = jnp.exp(jnp.where(causal, seg, -jnp.inf))
    cb = jnp.einsum("bclgn,bcsgn->bclsg", cc, bc)
    y_diag = jnp.einsum("bclsg,bclsgh,bcsghp->bclghp", cb, decay, xdt)
    decay_to_end = jnp.exp(acum[:, :, -1:] - acum)
    states = jnp.einsum("bclgn,bclgh,bclghp->bcghpn", bc, decay_to_end, xdt)
    chunk_decay = jnp.exp(acum[:, :, -1])

    def step(carry, inp):
        st, dec = inp
        new = carry * dec[..., None, None] + st
        return new, carry

    init = jnp.zeros((bsz, N_GROUPS, hg, hdim, D_STATE), jnp.float32)
    _, prev = lax.scan(step, init, (jnp.swapaxes(states, 0, 1), jnp.swapaxes(chunk_decay, 0, 1)))
    prev = jnp.swapaxes(prev, 0, 1)
    y_off = jnp.einsum("bclgn,bcghpn,bclgh->bclghp", cc, prev, jnp.exp(acum))
    return (y_diag + y_off).reshape(bsz, seqlen, n_heads, hdim)


def setup_inputs(seed: int = 0) -> dict:
    key = jax.random.key(seed)
    ks = jax.random.split(key, 32)

    def nrm(k, shape, scale):
        return jax.random.normal(k, shape, jnp.float32) * scale

    L = DEPTH
    dt_u = jax.random.uniform(ks[13], (L, N_HEADS_SSM), jnp.float32)
    dt0 = jnp.exp(dt_u * (math.log(0.1) - math.log(0.001)) + math.log(0.001))
    dt_bias = dt0 + jnp.log(-jnp.expm1(-dt0))
    return {
        "x": nrm(ks[0], (BATCH, SEQ, D_MODEL), 1.0),
        "c": nrm(ks[1], (BATCH, D_MODEL), 1.0),
        "w_ada": nrm(ks[2], (L, D_MODEL, N_MOD * D_MODEL), 0.5 * D_MODEL ** -0.5),
        "b_ada": nrm(ks[3], (L, N_MOD * D_MODEL), 0.02),
        "norm1_g": 1.0 + nrm(ks[4], (L, D_MODEL), 0.02),
        "w_in": nrm(ks[5], (L, D_MODEL, D_IN_PROJ), D_MODEL ** -0.5),
        "conv_a_w": nrm(ks[6], (L, KERNEL_A, D_CONV_A), KERNEL_A ** -0.5),
        "conv_a_b": nrm(ks[7], (L, D_CONV_A), 0.02),
        "ln_a_g": 1.0 + nrm(ks[8], (L, D_CONV_A), 0.02),
        "ln_a_b": nrm(ks[9], (L, D_CONV_A), 0.02),
        "w_a_out": nrm(ks[10], (L, D_CONV_A, D_MODEL), D_CONV_A ** -0.5),
        "b_a_out": nrm(ks[11], (L, D_MODEL), 0.02),
        "conv_ssm_w": nrm(ks[12], (L, KERNEL_SSM, D_XBC), KERNEL_SSM ** -0.5),
        "conv_ssm_b": nrm(ks[14], (L, D_XBC), 0.02),
        "dt_bias": dt_bias,
        "a_log": jnp.log(jax.random.uniform(ks[15], (L, N_HEADS_SSM), jnp.float32, 1.0, 16.0)),
        "d_skip": 1.0 + nrm(ks[16], (L, N_HEADS_SSM), 0.1),
        "ssm_norm_g": 1.0 + nrm(ks[17], (L, D_INNER), 0.02),
        "w_b_out": nrm(ks[18], (L, D_INNER, D_MODEL), D_INNER ** -0.5),
        "w_gate": nrm(ks[19], (L, D_MODEL, N_BRANCH * D_MODEL), D_MODEL ** -0.5),
        "b_gate": nrm(ks[20], (L, N_BRANCH * D_MODEL), 0.02),
        "w_o": nrm(ks[21], (L, D_MODEL, D_MODEL), D_MODEL ** -0.5),
        "norm2_g": 1.0 + nrm(ks[22], (L, D_MODEL), 0.02),
        "w_up": nrm(ks[23], (L, D_MODEL, 2 * D_FF), D_MODEL ** -0.5),
        "conv_ffn_w": nrm(ks[24], (L, KERNEL_FFN, 2 * D_FF), KERNEL_FFN ** -0.5),
        "conv_ffn_b": nrm(ks[25], (L, 2 * D_FF), 0.02),
        "w_down": nrm(ks[26], (L, D_FF, D_MODEL), D_FF ** -0.5),
        "norm_f_g": 1.0 + nrm(ks[27], (D_MODEL,), 0.02),
    }


def reference(x, c, w_ada, b_ada, norm1_g, w_in, conv_a_w, conv_a_b, ln_a_g, ln_a_b,
              w_a_out, b_a_out, conv_ssm_w, conv_ssm_b, dt_bias, a_log, d_skip,
              ssm_norm_g, w_b_out, w_gate, b_gate, w_o, norm2_g, w_up, conv_ffn_w,
              conv_ffn_b, w_down, norm_f_g):
    bsz, seqlen, _ = x.shape
    for i in range(DEPTH):
        mod = jax.nn.silu(c) @ w_ada[i] + b_ada[i]
        sh1, sc1, gt1, sh2, sc2, gt2 = jnp.split(mod, N_MOD, axis=-1)

        h = modulate(rmsnorm(x, norm1_g[i]), sh1, sc1)
        proj = h @ w_in[i]
        a_val, a_gate, z, xbc, dt_raw = jnp.split(
            proj, np.cumsum([D_CONV_A, D_CONV_A, D_INNER, D_XBC]).tolist(), axis=-1)

        u = a_val * jax.nn.sigmoid(a_gate)
        u = causal_dwconv(u, conv_a_w[i], conv_a_b[i])
        u = jax.nn.silu(layernorm(u, ln_a_g[i], ln_a_b[i]))
        out_a = u @ w_a_out[i] + b_a_out[i]

        xbc = jax.nn.silu(causal_dwconv(xbc, conv_ssm_w[i], conv_ssm_b[i]))
        xs, bm, cm = jnp.split(xbc, [D_INNER, D_INNER + N_GROUPS * D_STATE], axis=-1)
        xs_h = xs.reshape(bsz, seqlen, N_HEADS_SSM, HEAD_DIM_SSM)
        dt = jax.nn.softplus(dt_raw.astype(jnp.float32) + dt_bias[i].astype(jnp.float32))
        a = -jnp.exp(a_log[i].astype(jnp.float32))
        y = ssd_scan(xs_h, dt,  a,
                     bm.reshape(bsz, seqlen, N_GROUPS, D_STATE),
                     cm.reshape(bsz, seqlen, N_GROUPS, D_STATE))
        y = y + xs_h.astype(jnp.float32) * d_skip[i].astype(jnp.float32)[:, None]
        y = y.reshape(bsz, seqlen, D_INNER) * jax.nn.silu(z.astype(jnp.float32))
        yg = y.reshape(bsz, seqlen, N_GROUPS, D_INNER // N_GROUPS)
        yg = yg * lax.rsqrt(jnp.mean(yg * yg, axis=-1, keepdims=True) + EPS_SSM_NORM)
        y = (yg.reshape(bsz, seqlen, D_INNER) * ssm_norm_g[i].astype(jnp.float32)).astype(x.dtype)
        out_b = y @ w_b_out[i]

        g_a, g_b = jnp.split(jax.nn.sigmoid(h @ w_gate[i] + b_gate[i]), N_BRANCH, axis=-1)
        mix = (g_a * out_a + g_b * out_b) @ w_o[i]
        x = x + gt1[:, None, :] * mix

        h2 = modulate(rmsnorm(x, norm2_g[i]), sh2, sc2)
        up = causal_dwconv(h2 @ w_up[i], conv_ffn_w[i], conv_ffn_b[i])
        f_gate, f_val = jnp.split(up, 2, axis=-1)
        x = x + gt2[:, None, :] * ((jax.nn.silu(f_gate) * f_val) @ w_down[i])

    return rmsnorm(x, norm_f_g)
```

```python
import numpy as np
import concourse.bass as bass
import concourse.mybir as mybir

F32 = mybir.dt.float32
BF16 = mybir.dt.bfloat16
AF = mybir.ActivationFunctionType
ALU = mybir.AluOpType
AX = mybir.AxisListType

_ESZ = {F32: 4, BF16: 2}

SB_LO = 16512
SB_HI = 229344


class View:
    __slots__ = ("ap", "space", "base", "reg")

    def __init__(self, ap, space, base):
        self.ap = ap
        self.space = space
        self.base = base
        pat = ap.ap
        S = pat[0][0]
        off = ap.offset
        esz = _ESZ[ap.dtype]
        if S == 0:
            p0, lo = 0, off
        else:
            p0, lo = off // S, off % S
        hi = lo + 1
        for st, cnt in pat[1:]:
            hi += (cnt - 1) * st
        b0 = base + lo * esz
        b1 = base + hi * esz
        if space == "P":
            b0 = (b0 // 2048) * 2048
            b1 = ((b1 + 2047) // 2048) * 2048
            self.reg = ("P", 0, 128, b0, b1)
        else:
            self.reg = ("S", p0, p0 + pat[0][1], b0, b1)

    def __getitem__(self, key):
        return View(self.ap[key], self.space, self.base)

    def f(self, fn):
        return View(fn(self.ap), self.space, self.base)

    def bc(self, axis, n):
        a = self.ap.unsqueeze(axis)
        shp = list(a.shape)
        shp[axis] = n
        return View(a.to_broadcast(shp), self.space, self.base)

    def bitcast(self, dt):
        return View(self.ap.bitcast(dt), self.space, self.base)

    @property
    def shape(self):
        return self.ap.shape


class Buf:
    def __init__(self, prog, name, shape, dtype, space="S"):
        self.name = name
        self.shape = list(shape)
        self.dtype = dtype
        self.space = space
        nc = prog.nc
        free = int(np.prod(shape[1:]))
        nbytes = free * _ESZ[dtype]
        if space == "S":
            off = prog.sb_alloc(nbytes)
            self.base = off
            self.t = nc.alloc_sbuf_tensor_at(name, list(shape), dtype, offset=off)
        else:
            self.base = 0
            self.t = nc.alloc_psum_tensor(name, list(shape), dtype)
        self.nbytes = nbytes

    def __getitem__(self, key):
        return View(self.t[key], self.space, self.base)

    def alias(self, prog, name, shape, dtype, byte_off=0):
        b = Buf.__new__(Buf)
        b.name = name
        b.shape = list(shape)
        b.dtype = dtype
        b.space = "S"
        b.base = self.base + byte_off
        b.nbytes = int(np.prod(shape[1:])) * _ESZ[dtype]
        assert byte_off + b.nbytes <= self.nbytes, (name, byte_off, b.nbytes, self.nbytes)
        b.t = prog.nc.alloc_sbuf_tensor_at(name, list(shape), dtype, offset=b.base)
        return b


class Op:
    __slots__ = ("q", "fn", "deps", "dma_sem", "dma_val", "signal", "count", "idx")


QUEUES = ("pe", "act", "dve", "pool", "sp")


class Prog:
    def __init__(self, nc):
        self.nc = nc
        self.ops = []
        self.sb_ptr = SB_LO
        self.live = {}
        self.dma_counts = {}
        self.dma_sems = {}
        self.n_dma_sems = 0

    def sb_alloc(self, nbytes, align=64):
        off = (self.sb_ptr + align - 1) // align * align
        assert off + nbytes <= SB_HI, ("SBUF overflow", off, nbytes)
        self.sb_ptr = off + nbytes
        return off

    def sbuf(self, name, shape, dtype):
        return Buf(self, name, shape, dtype, "S")

    def psum(self, name, shape, dtype=F32):
        return Buf(self, name, shape, dtype, "P")

    @staticmethod
    def _buckets(reg):
        sp, p0, p1, b0, b1 = reg
        if sp == "P":
            return [("P", k) for k in range(b0 // 2048, (b1 + 2047) // 2048)]
        return [("S", k) for k in range(b0 // 1024, (b1 + 1023) // 1024)]

    @staticmethod
    def _overlap(a, b):
        return a[0] == b[0] and a[1] < b[2] and b[1] < a[2] and a[3] < b[4] and b[3] < a[4]

    @staticmethod
    def _covers(a, b):
        return a[1] <= b[1] and a[2] >= b[2] and a[3] <= b[3] and a[4] >= b[4]

    def _track(self, idx, q, reads, writes):
        deps = {}
        for r in reads:
            for bk in self._buckets(r):
                for rec in self.live.get(bk, ()):
                    if rec[2] and self._overlap(rec[0], r):
                        deps[rec[1]] = True
        for w in writes:
            for bk in self._buckets(w):
                for rec in self.live.get(bk, ()):
                    if self._overlap(rec[0], w):
                        if rec[1] != idx and rec[1] not in deps:
                            deps[rec[1]] = False
        for w in writes:
            for bk in self._buckets(w):
                lst = self.live.setdefault(bk, [])
                lst[:] = [rec for rec in lst if not self._covers(w, rec[0])]
                lst.append([w, idx, True, q])
        for r in reads:
            for bk in self._buckets(r):
                lst = self.live.setdefault(bk, [])
                lst[:] = [rec for rec in lst
                          if not ((not rec[2]) and rec[3] == q and self._covers(r, rec[0]))]
                lst.append([r, idx, False, q])
        deps.pop(idx, None)
        return deps

    def add(self, q, fn, reads=(), writes=(), dma_sem=None):
        op = Op()
        op.q = q
        op.fn = fn
        op.idx = len(self.ops)
        rr = [v.reg for v in reads if isinstance(v, View)]
        ww = [v.reg for v in writes if isinstance(v, View)]
        op.deps = self._track(op.idx, q, rr, ww)
        op.dma_sem = dma_sem
        op.dma_val = None
        if dma_sem is not None:
            self.dma_counts[dma_sem] = self.dma_counts.get(dma_sem, 0) + 16
            op.dma_val = self.dma_counts[dma_sem]
        op.signal = False
        op.count = None
        self.ops.append(op)
        return op

    def _eng(self, e, q):
        return e

    def mm(self, out, lhsT, rhs, start=True, stop=True, **kw):
        return self.add("pe", lambda e: e.matmul(out.ap, lhsT.ap, rhs.ap, start=start, stop=stop, **kw),
                        reads=[lhsT, rhs], writes=[out])

    def transpose(self, out, in_, ident):
        return self.add("pe", lambda e: e.transpose(out.ap, in_.ap, ident.ap),
                        reads=[in_, ident], writes=[out])

    def act(self, out, in_, func, bias=None, scale=None, accum_out=None, q="act"):
        kw = {}
        rd = [in_]
        wr = [out]
        if bias is not None:
            kw["bias"] = bias.ap if isinstance(bias, View) else bias
            rd.append(bias)
        if scale is not None:
            kw["scale"] = scale.ap if isinstance(scale, View) else scale
            rd.append(scale)
        if accum_out is not None:
            kw["accum_out"] = accum_out.ap
            wr.append(accum_out)
        return self.add(q, lambda e: e.activation(out.ap, in_.ap, func, **kw), reads=rd, writes=wr)

    def tt(self, q, out, in0, in1, op):
        return self.add(q, lambda e: e.tensor_tensor(out.ap, in0.ap, in1.ap, op),
                        reads=[in0, in1], writes=[out])

    def ts(self, q, out, in0, s1, op0, s2=None, op1=None, accum_out=None):
        rd = [in0, s1, s2]
        a1 = s1.ap if isinstance(s1, View) else s1
        a2 = s2.ap if isinstance(s2, View) else s2
        kw = {}
        wr = [out]
        if op1 is not None:
            kw["op1"] = op1
        if accum_out is not None:
            kw["accum_out"] = accum_out.ap
            wr.append(accum_out)
        return self.add(q, lambda e: e.tensor_scalar(out.ap, in0.ap, a1, a2, op0, **kw),
                        reads=rd, writes=wr)

    def stt(self, q, out, in0, scalar, in1, op0, op1):
        a = scalar.ap if isinstance(scalar, View) else scalar
        return self.add(q, lambda e: e.scalar_tensor_tensor(out.ap, in0.ap, a, in1.ap, op0, op1),
                        reads=[in0, scalar, in1], writes=[out])

    def copy(self, q, out, in_):
        if q == "act":
            return self.add(q, lambda e: e.copy(out.ap, in_.ap), reads=[in_], writes=[out])
        return self.add(q, lambda e: e.tensor_copy(out.ap, in_.ap), reads=[in_], writes=[out])

    def memset(self, q, out, val):
        return self.add(q, lambda e: e.memset(out.ap, val), reads=[], writes=[out])

    def dma(self, q, out, in_, sem):
        o = out.ap if isinstance(out, View) else out
        i = in_.ap if isinstance(in_, View) else in_
        if sem not in self.dma_sems:
            self.dma_sems[sem] = None
        return self.add(q, lambda e: e.dma_start(out=o, in_=i),
                        reads=[in_] if isinstance(in_, View) else [],
                        writes=[out] if isinstance(out, View) else [],
                        dma_sem=sem)

    def emit(self, final_waits=()):
        nc = self.nc
        ops = self.ops
        need = []
        for op in ops:
            best = {}
            for j, is_raw in op.deps.items():
                pj = ops[j]
                if pj.dma_sem is not None:
                    key = ("d", pj.dma_sem)
                    if key not in best or best[key].idx < pj.idx:
                        best[key] = pj
                    continue
                if pj.q == op.q:
                    if op.q == "pe":
                        continue
                key = ("c", pj.q)
                if key not in best or best[key].idx < pj.idx:
                    best[key] = pj
            for key, pj in best.items():
                if key[0] == "c":
                    pj.signal = True
            need.append(best)
        cnt = {q: 0 for q in QUEUES}
        for op in ops:
            if op.signal:
                cnt[op.q] += 1
                op.count = cnt[op.q]
        self.sig_counts = dict(cnt)
        from contextlib import ExitStack
        with ExitStack() as es:
            csem = {q: es.enter_context(nc.semaphore("c_" + q)) for q in QUEUES}
            dsem = {s: es.enter_context(nc.semaphore("d_" + s)) for s in self.dma_sems}
            block = es.enter_context(nc.Block())
            per_q = {q: [] for q in QUEUES}
            for op, nd in zip(ops, need):
                per_q[op.q].append((op, nd))

            def run_queue(q, eng):
                waited = {}
                for op, nd in per_q[q]:
                    for key, pj in nd.items():
                        if key[0] == "d":
                            sem, val = dsem[key[1]], pj.dma_val
                        else:
                            sem, val = csem[key[1]], pj.count
                        if waited.get(key, 0) >= val:
                            continue
                        waited[key] = val
                        eng.wait_ge(sem, val)
                    ins = op.fn(eng)
                    if op.dma_sem is not None:
                        ins.then_inc(dsem[op.dma_sem], 16)
                    elif op.signal:
                        ins.then_inc(csem[q], 1)
                if q == "sp":
                    for s in final_waits:
                        eng.wait_ge(dsem[s], self.dma_counts[s])

            @block.tensor
            def _(e):
                run_queue("pe", e)

            @block.scalar
            def _(e):
                run_queue("act", e)

            @block.vector
            def _(e):
                run_queue("dve", e)

            @block.gpsimd
            def _(e):
                run_queue("pool", e)

            @block.sync
            def _(e):
                run_queue("sp", e)


D = 1024
S_FULL = 2048
TB = 512
KB = 1024
EPS = 1e-6
EPS_SSM = 1e-5

V_CT, V_BADA, V_N1G, V_CAB, V_LNG, V_LNB, V_BAO, V_BG, V_N2G, V_CFB, V_NFG, V_CBCB = \
    0, 8, 56, 64, 72, 80, 88, 96, 112, 120, 164, 172
V_CAW = 180
V_CSW = V_CAW + 248
V_CFW = V_CSW + 96
V_CSB = V_CFW + 132
NV = V_CSB + 24

C_ID, C_L2, C_TRT, C_BON, C_OTOP, C_OBOT, C_TRI2 = 0, 128, 256, 384, 512, 640, 768
NCST = 768 + 64


def _units_fm(W, starts, G, KT):
    nu = len(starts) // G
    out = np.empty((nu, 128, G, KT, 128), np.float32)
    for i, s in enumerate(starts):
        blk = W[:, s:s + 128].reshape(KT, 128, 128)
        out[i // G, :, i % G] = blk.transpose(1, 0, 2)
    return out.reshape(nu, 128, G * KT * 128)


def _fm_vec(v):
    return np.ascontiguousarray(v.reshape(-1, 128).T)


def host_prep(inp):
    f = lambda k: np.asarray(inp[k], np.float32)
    w_in = f("w_in")[0]
    sh = {}
    wada = f("w_ada")[0]
    sh["wada"] = _units_fm(wada, [i * 128 for i in range(48)], 4, 8)
    ag_starts = []
    for c in range(8):
        ag_starts += [1024 + c * 128, c * 128]
    sh["wag"] = _units_fm(w_in, ag_starts, 4, 8)
    sh["wxs"] = _units_fm(w_in, [4096 + i * 128 for i in range(16)], 4, 8)
    sh["wbc"] = _units_fm(w_in, [6144 + i * 128 for i in range(8)], 4, 8)
    wz = w_in[:, 2048:4096].reshape(8, 128, 4, 512)
    sh["wz"] = np.ascontiguousarray(wz.transpose(2, 1, 0, 3)).reshape(4, 128, 4096)
    sh["wdt"] = np.ascontiguousarray(w_in[:, 7168:7200].reshape(8, 128, 32).transpose(1, 0, 2)).reshape(128, 256)
    sh["wgate"] = _units_fm(f("w_gate")[0], [i * 128 for i in range(16)], 4, 8)
    sh["waout"] = _units_fm(f("w_a_out")[0], [i * 128 for i in range(8)], 4, 8)
    sh["wbout"] = _units_fm(f("w_b_out")[0], [i * 128 for i in range(8)], 2, 16)
    sh["wo"] = _units_fm(f("w_o")[0], [i * 128 for i in range(8)], 4, 8)
    up_starts = []
    for j in range(22):
        up_starts += [j * 128, 2816 + j * 128]
    sh["wup"] = _units_fm(f("w_up")[0], up_starts, 4, 8)
    sh["wdown"] = _units_fm(f("w_down")[0], [i * 128 for i in range(8)], 1, 22)
    vec = np.zeros((128, NV), np.float32)
    vec[:, V_BADA:V_BADA + 48] = _fm_vec(f("b_ada")[0])
    vec[:, V_N1G:V_N1G + 8] = _fm_vec(f("norm1_g")[0])
    vec[:, V_CAB:V_CAB + 8] = _fm_vec(f("conv_a_b")[0])
    vec[:, V_LNG:V_LNG + 8] = _fm_vec(f("ln_a_g")[0])
    vec[:, V_LNB:V_LNB + 8] = _fm_vec(f("ln_a_b")[0])
    vec[:, V_BAO:V_BAO + 8] = _fm_vec(f("b_a_out")[0])
    vec[:, V_BG:V_BG + 16] = _fm_vec(f("b_gate")[0])
    vec[:, V_N2G:V_N2G + 8] = _fm_vec(f("norm2_g")[0])
    vec[:, V_CFB:V_CFB + 44] = _fm_vec(f("conv_ffn_b")[0])
    vec[:, V_NFG:V_NFG + 8] = _fm_vec(f("norm_f_g"))
    csb = f("conv_ssm_b")[0]
    vec[:, V_CSB:V_CSB + 24] = _fm_vec(csb)
    caw = f("conv_a_w")[0]
    vec[:, V_CAW:V_CAW + 248] = caw.T.reshape(8, 128, 31).transpose(1, 0, 2).reshape(128, 248)
    csw = f("conv_ssm_w")[0]
    vec[:, V_CSW:V_CSW + 96] = csw.T.reshape(24, 128, 4).transpose(1, 0, 2).reshape(128, 96)
    cfw = f("conv_ffn_w")[0]
    vec[:, V_CFW:V_CFW + 132] = cfw.T.reshape(44, 128, 3).transpose(1, 0, 2).reshape(128, 132)
    tmc = np.zeros((128, 96), np.float32)
    tmc[:, 0:32] = f("dt_bias")[0][None, :]
    tmc[:, 32:64] = f("a_log")[0][None, :]
    tmc[:, 64:96] = f("d_skip")[0][None, :]
    sh["tmc"] = tmc
    sh["gnb"] = np.ascontiguousarray(np.broadcast_to(f("ssm_norm_g")[0][None, :], (128, 2048)))
    cst = np.zeros((128, NCST), np.float32)
    idx = np.arange(128)
    same = (idx[:, None] // 64) == (idx[None, :] // 64)
    cst[:, C_ID:C_ID + 128] = np.eye(128)
    cst[:, C_L2:C_L2 + 128] = same & (idx[:, None] > idx[None, :])
    cst[:, C_TRT:C_TRT + 128] = same & (idx[:, None] <= idx[None, :])
    cst[:, C_BON:C_BON + 128] = same
    cst[:, C_OTOP:C_OTOP + 128] = (idx[:, None] < 64)
    cst[:, C_OBOT:C_OBOT + 128] = (idx[:, None] >= 64)
    cst[:, C_TRI2:C_TRI2 + 64] = ((idx[:, None] % 64) <= np.arange(64)[None, :])
    sh["cst"] = cst
    x = f("x")
    c = f("c")
    per = []
    for b in range(x.shape[0]):
        v = vec.copy()
        v[:, V_CT:V_CT + 8] = _fm_vec(c[b])
        per.append({"xT": np.ascontiguousarray(x[b].T), "vec": v})
    return sh, per


def build(S):
    NBLK = S // TB
    nc = bass.Bass("TRN2", target_bir_lowering=False)
    pg = Prog(nc)

    def din(name, shape):
        return nc.dram_tensor(name, list(shape), F32, kind="ExternalInput").ap()

    xT = din("xT", [D, S])
    vec_d = din("vec", [128, NV])
    cst_d = din("cst", [128, NCST])
    tmc_d = din("tmc", [128, 96])
    gnb_d = din("gnb", [128, 2048])
    wdt_d = din("wdt", [128, 256])
    wd = {k: din(k, shp) for k, shp in [
        ("wada", [12, 128, 4096]), ("wag", [4, 128, 4096]), ("wxs", [4, 128, 4096]),
        ("wbc", [2, 128, 4096]), ("wz", [4, 128, 4096]), ("wgate", [4, 128, 4096]),
        ("waout", [2, 128, 4096]), ("wbout", [4, 128, 4096]), ("wo", [2, 128, 4096]),
        ("wup", [11, 128, 4096]), ("wdown", [8, 128, 2816])]}
    outT = nc.dram_tensor("outT", [D, S], F32, kind="ExternalOutput").ap()

    base = SB_LO

    def at(name, shape, dtype, off):
        b = Buf.__new__(Buf)
        b.name, b.shape, b.dtype, b.space = name, list(shape), dtype, "S"
        b.base = base + off
        b.nbytes = int(np.prod(shape[1:])) * _ESZ[dtype]
        assert b.base % 32 == 0 and b.base + b.nbytes <= SB_HI, (name, off, b.nbytes)
        b.t = nc.alloc_sbuf_tensor_at(name, list(shape), dtype, offset=b.base)
        b.end = off + b.nbytes
        return b

    class Seq:
        def __init__(self, start, limit):
            self.p, self.limit = start, limit

        def __call__(self, name, shape, dtype):
            off = (self.p + 63) // 64 * 64
            b = at(name, shape, dtype, off)
            self.p = b.end
            assert self.p <= self.limit, ("region overflow", name, self.p, self.limit)
            return b

    P = Seq(0, 67 * KB)
    cst = P("cst", [128, NCST], F32)
    vec = P("vec", [128, NV], F32)
    tmc = P("tmc", [128, 96], F32)
    idb = P("idb", [128, 128], BF16)
    onb = P("onb", [128, 128], BF16)
    onf = P("onf", [128, 128], F32)
    modv = P("modv", [128, 48], F32)
    gs = P("gs", [128, 16], F32)
    abc = P("abc", [128, 32], F32)
    dbc = P("dbc", [128, 32], BF16)
    scb = P("scb", [128, 8], BF16)
    wdt = P("wdt", [128, 8, 32], BF16)
    gnb = P("gnb", [128, 2048], BF16)
    halo_u = P("halo_u", [128, 8, 30], BF16)
    halo_x = P("halo_x", [128, 24, 3], BF16)
    halo_f = P("halo_f", [128, 44, 2], BF16)
    Sst = P("Sst", [128, 4, 512], F32)
    SbA = [P("SbA0", [128, 4, 512], BF16), P("SbA1", [128, 4, 512], BF16)]
    SbB = P("SbB", [128, 4, 512], BF16)
    NSLOT = 3
    wslots = [P("wsl%d" % i, [128, 4096], BF16) for i in range(NSLOT)]
    hb = P("hb", [128, 8, 512], BF16)
    assert P.p <= 67 * KB, P.p
    M = Seq(67 * KB, 143 * KB)
    sz = M("sz", [128, 4, 2048], BF16)
    xsT = M("xsT", [128, 4, 2048], BF16)
    BT = M("BT", [128, 4, 512], BF16)
    BfT = M("BfT", [128, 4, 512], BF16)
    CfT = M("CfT", [128, 4, 512], BF16)
    C0 = M("C0", [128, 4, 4, 128], BF16)
    C1 = M("C1", [128, 4, 4, 128], BF16)
    ua = M("ua", [128, 8, 512], F32)
    ynT = at("ynT", [128, 16, 512], BF16, ua.base - base)
    uA = M("uA", [128, 8, 512], BF16)
    hid = at("hid", [128, 22, 512], BF16, 67 * KB)
    A0 = 143 * KB
    A_END = SB_HI - base
    xb = at("xb", [128, 8, 512], F32, A_END - 16 * KB - 64)
    A_LIM = A_END - 16 * KB - 64
    G_ = Seq(A0, A_LIM)
    sq = G_("sq", [128, 8, 512], BF16)
    st0 = G_("st0", [128, 512], F32)
    st1 = G_("st1", [128, 512], F32)
    st2 = G_("st2", [128, 512], F32)
    tmpf = [G_("tmpf0", [128, 512], F32), G_("tmpf1", [128, 512], F32)]
    NDIAG = 20
    dgs = [G_("dg%d" % i, [128, 128], BF16) for i in range(NDIAG)]
    g_end = G_.p
    Q = Seq(g_end, A_END)
    ub = Q("ub", [128, 8, 542], BF16)
    xbcp = Q("xbcp", [128, 24, 515], BF16)
    sgs = [Q("sg0", [128, 512], BF16), Q("sg1", [128, 512], BF16)]
    Q2 = Seq(g_end, A_LIM)
    gab = Q2("gab", [128, 16, 512], BF16)
    t1 = Q2("t1", [128, 8, 512], BF16)
    Q3 = Seq(g_end, A_LIM)
    pgs = [Q3("pgs%d" % i, [128, 514], BF16) for i in range(4)]
    fgs = [Q3("fg%d" % i, [128, 512], BF16) for i in range(2)]
    ost = [Q3("ost%d" % i, [128, 512], F32) for i in range(2)]
    R = Seq(A0, A_END)
    R2 = R("R2", [128, 32, 64], F32)
    eb = R("eb", [128, 32, 64], BF16)
    m2 = R("m2", [128, 32, 128], BF16)
    xdt = R("xdt", [128, 32, 64], BF16)
    xw = R("xw", [128, 32, 64], BF16)
    xsD = R("xsD", [128, 32, 64], BF16)
    tyo = R("tyo", [128, 4, 512], F32)
    sqf = R("sqf", [128, 4, 512], F32)
    yn = R("yn", [128, 2048], BF16)
    CBm = R("CBm", [128, 4, 128], BF16)
    CBc = R("CBc", [128, 4, 64], BF16)
    sm = R("sm", [128, 16, 32], F32)

    ps = pg.psum("ps", [128, 4096], F32)
    bank_ctr = [0]

    def bank():
        i = bank_ctr[0] % 3
        bank_ctr[0] += 1
        return ps[:, 512 * i:512 * (i + 1)]

    def fixed_bank(i):
        return ps[:, 512 * i:512 * (i + 1)]

    ident_f = cst[:, C_ID:C_ID + 128]
    L2 = cst[:, C_L2:C_L2 + 128]
    trT = cst[:, C_TRT:C_TRT + 128]
    bones = cst[:, C_BON:C_BON + 128]
    otop = cst[:, C_OTOP:C_OTOP + 128]
    obot = cst[:, C_OBOT:C_OBOT + 128]
    tri2 = cst[:, C_TRI2:C_TRI2 + 64]

    def vcol(c0, i=0):
        return vec[:, c0 + i:c0 + i + 1]

    sched = [("wada", u) for u in range(12)]
    blk_sched = ([("wag", u) for u in range(4)] + [("wxs", u) for u in range(4)] +
                 [("wbc", u) for u in range(2)] + [("wz", u) for u in range(4)] +
                 [("wgate", u) for u in range(4)] + [("waout", u) for u in range(2)] +
                 [("wbout", u) for u in range(4)] + [("wo", u) for u in range(2)] +
                 [("wup", u) for u in range(11)] + [("wdown", u) for u in range(8)])
    for _ in range(NBLK):
        sched += blk_sched
    wstate = {"issued": 0, "used": 0}

    def w_issue_upto(n):
        while wstate["issued"] < min(n, len(sched)):
            i = wstate["issued"]
            name, u = sched[i]
            ncol = 2816 if name == "wdown" else 4096
            slot = wslots[i % NSLOT]
            pg.dma("pool", slot[:, 0:ncol], wd[name][u], "w%d" % (i % NSLOT))
            wstate["issued"] += 1

    def wget(name):
        i = wstate["used"]
        assert sched[i][0] == name, (sched[i], name)
        w_issue_upto(i + NSLOT)
        wstate["used"] += 1
        return wslots[i % NSLOT]

    def w4(slot, G, KT):
        return slot[:, 0:G * KT * 128].f(lambda a: a.rearrange("p (g k m) -> p g k m", g=G, k=KT))

    pg.dma("sp", cst[:, :], cst_d, "c0")
    pg.dma("sp", vec[:, :], vec_d, "c1")
    pg.dma("sp", tmc[:, :], tmc_d, "c2")
    pg.dma("pool", gnb[:, :], gnb_d, "c3")
    pg.dma("pool", wdt[:, :, :].f(lambda a: a.rearrange("p k n -> p (k n)")), wdt_d, "c4")
    w_issue_upto(NSLOT)
    pg.copy("dve", idb[:, :], ident_f)
    pg.memset("dve", onb[:, :], 1.0)
    pg.memset("dve", onf[:, :], 1.0)
    pg.memset("pool", halo_u[:, :, :], 0.0)
    pg.memset("pool", halo_x[:, :, :], 0.0)
    pg.memset("pool", halo_f[:, :, :], 0.0)
    pg.memset("pool", Sst[:, :, :], 0.0)
    pg.memset("pool", SbA[0][:, :, :], 0.0)
    pg.memset("pool", C0[:, :, :, :], 0.0)
    pg.memset("pool", C1[:, :, :, :], 0.0)
    pg.act(abc[:, :], tmc[:, 32:64], AF.Exp)
    pg.ts("dve", abc[:, :], abc[:, :], -1.0, ALU.mult)
    pg.copy("dve", dbc[:, :], tmc[:, 64:96])
    pg.act(scb[:, :], vec[:, V_CT:V_CT + 8], AF.Silu)
    psm = fixed_bank(3)
    for u in range(12):
        sl = w4(wget("wada"), 4, 8)
        for g in range(4):
            oc = 4 * u + g
            for kt in range(8):
                pg.mm(psm[:, oc:oc + 1], sl[:, g, kt, :], scb[:, kt:kt + 1], start=(kt == 0), stop=(kt == 7))
    pg.tt("dve", modv[:, :], psm[:, 0:48], vec[:, V_BADA:V_BADA + 48], ALU.add)
    pg.stt("dve", gs[:, 0:8], modv[:, 8:16], 1.0, vec[:, V_N1G:V_N1G + 8], ALU.add, ALU.mult)
    pg.stt("dve", gs[:, 8:16], modv[:, 32:40], 1.0, vec[:, V_N2G:V_N2G + 8], ALU.add, ALU.mult)

    evac_ctr = [0]

    def evac_copy(out, in_):
        q = "act" if evac_ctr[0] % 2 == 0 else "dve"
        evac_ctr[0] += 1
        pg.copy(q, out, in_)

    dg_ctr = [0]

    def diag(col):
        d = dgs[dg_ctr[0] % NDIAG]
        dg_ctr[0] += 1
        pg.ts("pool", d[:, :], idb[:, :], col, ALU.mult)
        return d[:, :]

    def rms_stats(xbuf):
        pst = fixed_bank(3)
        for ft in range(8):
            pg.act(sq[:, ft, :], xbuf[:, ft, :], AF.Square)
        for ft in range(8):
            pg.mm(pst, onb[:, :], sq[:, ft, :], start=(ft == 0), stop=(ft == 7))
        pg.act(st0[:, :], pst, AF.Sqrt, bias=EPS, scale=1.0 / D)
        pg.add("dve", lambda e: e.reciprocal(st1[:, :].ap, st0[:, :].ap), reads=[st0[:, :]], writes=[st1[:, :]])
        return st1[:, :]

    def mod_norm(xbuf, gcol0, shcol0, out):
        rstd = rms_stats(xbuf)
        for ft in range(8):
            t = tmpf[ft % 2]
            pg.tt("dve", t[:, :], xbuf[:, ft, :], rstd, ALU.mult)
            pg.act(out[:, ft, :], t[:, :], AF.Identity, bias=modv[:, shcol0 + ft:shcol0 + ft + 1],
                   scale=gs[:, gcol0 + ft:gcol0 + ft + 1])

    xT_v = xT.rearrange("(f p) s -> p f s", p=128)
    outT_v = outT.rearrange("(f p) s -> p f s", p=128)

    for blk in range(NBLK):
        t0 = blk * TB
        pg.dma("sp", xb[:, :, :], xT_v[:, :, t0:t0 + TB], "x")
        mod_norm(xb, 0, 0, hb)
        pg.copy("pool", ub[:, :, 0:30], halo_u[:, :, :])
        pg.copy("pool", xbcp[:, :, 0:3], halo_x[:, :, :])
        for u in range(4):
            sl = w4(wget("wag"), 4, 8)
            for half in range(2):
                c = 2 * u + half
                pa = bank()
                for kt in range(8):
                    pg.mm(pa, sl[:, 2 * half, kt, :], hb[:, kt, :], start=(kt == 0), stop=(kt == 7))
                sg = sgs[c % 2]
                pg.act(sg[:, :], pa, AF.Sigmoid)
                pv = bank()
                for kt in range(8):
                    pg.mm(pv, sl[:, 2 * half + 1, kt, :], hb[:, kt, :], start=(kt == 0), stop=(kt == 7))
                pg.tt("dve", ub[:, c, 30:542], pv, sg[:, :], ALU.mult)
        for (nm, nu, ct0) in (("wxs", 4, 0), ("wbc", 2, 16)):
            for u in range(nu):
                sl = w4(wget(nm), 4, 8)
                for g in range(4):
                    ct = ct0 + 4 * u + g
                    pp = bank()
                    for kt in range(8):
                        pg.mm(pp, sl[:, g, kt, :], hb[:, kt, :], start=(kt == 0), stop=(kt == 7))
                    evac_copy(xbcp[:, ct, 3:515], pp)
        for u in range(4):
            sl = wget("wz")[:, :].f(lambda a: a.rearrange("p (k n) -> p k n", k=8))
            for pr in range(4):
                pz = bank()
                for kt in range(8):
                    pg.mm(pz, hb[:, kt, pr * 128:(pr + 1) * 128], sl[:, kt, :], start=(kt == 0), stop=(kt == 7))
                pg.act(sz[:, pr, u * 512:(u + 1) * 512], pz, AF.Silu)
        for cg in range(6):
            dlist = []
            for j in range(4):
                ct = 4 * cg + j
                dl = [diag(vcol(V_CSW, ct * 4 + k)) for k in range(4)]
                if cg < 5:
                    dl.append(diag(vcol(V_CSB, ct)))
                dlist.append(dl)
            if cg < 5:
                dst = xsT if cg < 4 else BT
                for pr in range(4):
                    pt = bank()
                    for j in range(4):
                        ct = 4 * cg + j
                        o = pt[:, j * 128:(j + 1) * 128]
                        for k in range(4):
                            pg.mm(o, xbcp[:, ct, pr * 128 + k:pr * 128 + k + 128], dlist[j][k],
                                  start=(k == 0), stop=False)
                        pg.mm(o, onb[:, :], dlist[j][4], start=False, stop=True)
                    if cg < 4:
                        pg.act(xsT[:, pr, cg * 512:(cg + 1) * 512], pt, AF.Silu)
                    else:
                        pg.act(BT[:, pr, :], pt, AF.Silu)
            if cg >= 4:
                dstf = BfT if cg == 4 else CfT
                for j in range(4):
                    ct = 4 * cg + j
                    pf = bank()
                    for k in range(4):
                        pg.mm(pf, dlist[j][k], xbcp[:, ct, k:k + 512], start=(k == 0), stop=(k == 3))
                    pg.act(dstf[:, j, :], pf, AF.Silu, bias=vcol(V_CSB, ct))
                    if cg == 5:
                        cv = CfT[:, j, :].f(lambda a: a.rearrange("p (r c l) -> p r c l", r=4, c=2))
                        pg.copy("pool", C0[:, j, :, 0:64], cv[:, :, 0, :])
                        pg.copy("pool", C1[:, j, :, 64:128], cv[:, :, 1, :])
        pg.copy("pool", halo_x[:, :, :], xbcp[:, :, 512:515])
        ps1 = fixed_bank(3)
        for ct in range(8):
            pc = bank()
            for k in range(31):
                dk = diag(vcol(V_CAW, ct * 31 + k))
                pg.mm(pc, dk, ub[:, ct, k:k + 512], start=(k == 0), stop=(k == 30))
            pg.act(ua[:, ct, :], pc, AF.Identity, bias=vcol(V_CAB, ct))
            pg.act(sq[:, ct, :], pc, AF.Square, bias=vcol(V_CAB, ct))
        pg.copy("pool", halo_u[:, :, :], ub[:, :, 512:542])
        for ct in range(8):
            pg.mm(ps1, onf[:, :], ua[:, ct, :], start=(ct == 0), stop=(ct == 7))
        ps2 = bank()
        for ct in range(8):
            pg.mm(ps2, onb[:, :], sq[:, ct, :], start=(ct == 0), stop=(ct == 7))
        pg.ts("dve", st0[:, :], ps1, 1.0 / D, ALU.mult)
        pg.tt("dve", st2[:, :], st0[:, :], st0[:, :], ALU.mult)
        pg.stt("dve", st2[:, :], ps2, 1.0 / D, st2[:, :], ALU.mult, ALU.subtract)
        pg.act(st2[:, :], st2[:, :], AF.Sqrt, bias=EPS)
        pg.add("dve", lambda e: e.reciprocal(st1[:, :].ap, st2[:, :].ap), reads=[st2[:, :]], writes=[st1[:, :]])
        for ct in range(8):
            t = tmpf[ct % 2]
            pg.tt("dve", t[:, :], ua[:, ct, :], st0[:, :], ALU.subtract)
            pg.tt("dve", t[:, :], t[:, :], st1[:, :], ALU.mult)
            pg.act(uA[:, ct, :], t[:, :], AF.Silu, bias=vcol(V_LNB, ct), scale=vcol(V_LNG, ct))

        pg.memset("pool", m2[:, :, :], 0.0)
        for pr in range(4):
            gp = blk * 4 + pr
            tp = pr * 128
            SA_in = SbA[gp % 2]
            SA_out = SbA[(gp + 1) % 2]
            pm = fixed_bank(3)
            for kt in range(8):
                pg.mm(pm[:, 0:32], hb[:, kt, tp:tp + 128], wdt[:, kt, :], start=(kt == 0), stop=(kt == 7))
            s_xb, s_ex, s_dt, s_da, s_eA, s_d2, s_A, s_w, s_cd0, s_cd1, s_ss, s_rs = [sm[:, i, :] for i in range(12)]
            pg.tt("dve", s_xb, pm[:, 0:32], tmc[:, 0:32], ALU.add)
            pg.act(s_ex, s_xb, AF.Exp)
            pg.act(s_dt, s_ex, AF.Ln, bias=1.0)
            pg.tt("dve", s_da, s_dt, abc[:, :], ALU.mult)
            pg.mm(pm[:, 32:64], trT, s_da)
            pg.mm(pm[:, 64:96], bones, s_da)
            pg.mm(pm[:, 96:128], otop, s_da)
            pg.mm(pm[:, 128:160], obot, s_da)
            pg.act(s_eA, pm[:, 32:64], AF.Exp)
            pg.copy("act", s_A, pm[:, 32:64])
            pg.tt("dve", s_d2, pm[:, 64:96], s_A, ALU.subtract)
            pg.act(s_d2, s_d2, AF.Exp)
            pg.tt("dve", s_w, s_dt, s_d2, ALU.mult)
            pg.act(s_cd0, pm[:, 96:128], AF.Exp)
            pg.act(s_cd1, pm[:, 128:160], AF.Exp)
            xs3 = xsT[:, pr, :].f(lambda a: a.rearrange("p (h d) -> p h d", h=32))
            pg.tt("dve", xdt[:, :, :], xs3, s_dt.bc(2, 64), ALU.mult)
            pg.tt("dve", xw[:, :, :], xs3, s_w.bc(2, 64), ALU.mult)
            pg.tt("dve", xsD[:, :, :], xs3, dbc[:, :].bc(2, 64), ALU.mult)
            for j in range(2):
                cdj = s_cd0 if j == 0 else s_cd1
                Sb_out = SbB if j == 0 else SA_out
                for g in range(4):
                    pd = bank()
                    pg.mm(pd, BT[64 * j:64 * j + 64, pr, g * 128:(g + 1) * 128],
                          xw[64 * j:64 * j + 64, g * 8:(g + 1) * 8, :].f(lambda a: a.rearrange("p h d -> p (h d)")))
                    S3 = Sst[:, g, :].f(lambda a: a.rearrange("p (h d) -> p h d", h=8))
                    pg.tt("pool", S3, S3, cdj[:, g * 8:(g + 1) * 8].bc(2, 64), ALU.mult)
                    pg.tt("dve", Sst[:, g, :], Sst[:, g, :], pd, ALU.add)
                    pg.copy("act", Sb_out[:, g, :], Sst[:, g, :])
            pg.tt("dve", R2[:, :, :], tri2.bc(1, 32), s_da.bc(2, 64), ALU.mult)
            R2f = R2[:, :, :].f(lambda a: a.rearrange("p h l -> p (h l)"))
            ebf = eb[:, :, :].f(lambda a: a.rearrange("p h l -> p (h l)"))
            for q in range(4):
                pg.mm(fixed_bank(4 + q), L2, R2f[:, q * 512:(q + 1) * 512])
            for q in range(4):
                pg.act(ebf[:, q * 512:(q + 1) * 512], fixed_bank(4 + q), AF.Exp)
            pcb = bank()
            for g in range(4):
                pg.mm(pcb[:, g * 128:(g + 1) * 128], BfT[:, g, tp:tp + 128], CfT[:, g, tp:tp + 128])
            pg.tt("dve", CBm[:, :, :], pcb.f(lambda a: a.rearrange("p (g l) -> p g l", g=4)), trT.bc(1, 4), ALU.mult)
            pg.tt("dve", CBc[:, :, :], CBm[:, :, 0:64], CBm[:, :, 64:128], ALU.add)
            for g in range(4):
                for j in range(2):
                    pg.tt("dve", m2[64 * j:64 * j + 64, g * 8:(g + 1) * 8, 64 * j:64 * j + 64],
                          eb[64 * j:64 * j + 64, g * 8:(g + 1) * 8, :],
                          CBc[64 * j:64 * j + 64, g, :].bc(1, 8), ALU.mult)
            for h in range(32):
                g = h // 8
                o = fixed_bank(4 + g)[:, (h % 8) * 64:(h % 8) * 64 + 64]
                pg.mm(o, m2[:, h, :], xdt[:, h, :], start=True, stop=False)
                pg.mm(o, idb[:, :], xsD[:, h, :], start=False, stop=True)
            for g in range(4):
                po = bank()
                pg.mm(po, C0[:, g, pr, :], SA_in[:, g, :], start=True, stop=False)
                pg.mm(po, C1[:, g, pr, :], SbB[:, g, :], start=False, stop=True)
                ty3 = tyo[:, g, :].f(lambda a: a.rearrange("p (h d) -> p h d", h=8))
                pg.tt("dve", ty3, po.f(lambda a: a.rearrange("p (h d) -> p h d", h=8)),
                      s_eA[:, g * 8:(g + 1) * 8].bc(2, 64), ALU.mult)
                pg.tt("dve", tyo[:, g, :], tyo[:, g, :], fixed_bank(4 + g), ALU.add)
                pg.tt("dve", tyo[:, g, :], tyo[:, g, :], sz[:, pr, g * 512:(g + 1) * 512], ALU.mult)
                pg.act(sqf[:, g, :], tyo[:, g, :], AF.Square)
            pg.add("dve", lambda e: e.tensor_reduce(s_ss[:, 0:4].ap, sqf[:, :, :].ap, AX.X, ALU.add),
                   reads=[sqf[:, :, :]], writes=[s_ss[:, 0:4]])
            pg.act(s_ss[:, 0:4], s_ss[:, 0:4], AF.Sqrt, bias=EPS_SSM, scale=1.0 / 512)
            pg.add("dve", lambda e: e.reciprocal(s_rs[:, 0:4].ap, s_ss[:, 0:4].ap),
                   reads=[s_ss[:, 0:4]], writes=[s_rs[:, 0:4]])
            for g in range(4):
                pg.stt("dve", yn[:, g * 512:(g + 1) * 512], tyo[:, g, :], s_rs[:, g:g + 1],
                       gnb[:, g * 512:(g + 1) * 512], ALU.mult, ALU.mult)
            for c4 in range(4):
                ptb = bank().bitcast(BF16)
                for j in range(4):
                    ct = 4 * c4 + j
                    pg.transpose(ptb[:, j * 128:(j + 1) * 128], yn[:, ct * 128:(ct + 1) * 128], idb[:, :])
                evac_copy(ynT[:, 4 * c4:4 * c4 + 4, tp:tp + 128],
                          ptb[:, 0:512].f(lambda a: a.rearrange("p (c t) -> p c t", c=4)))

        for u in range(4):
            sl = w4(wget("wgate"), 4, 8)
            for g in range(4):
                oc = 4 * u + g
                pq = bank()
                for kt in range(8):
                    pg.mm(pq, sl[:, g, kt, :], hb[:, kt, :], start=(kt == 0), stop=(kt == 7))
                pg.act(gab[:, oc, :], pq, AF.Sigmoid, bias=vcol(V_BG, oc))
        for u in range(2):
            sl = w4(wget("waout"), 4, 8)
            for g in range(4):
                oc = 4 * u + g
                pq = bank()
                for kt in range(8):
                    pg.mm(pq, sl[:, g, kt, :], uA[:, kt, :], start=(kt == 0), stop=(kt == 7))
                pg.stt("dve", t1[:, oc, :], pq, vcol(V_BAO, oc), gab[:, oc, :], ALU.add, ALU.mult)
        for u in range(4):
            sl = w4(wget("wbout"), 2, 16)
            for g in range(2):
                oc = 2 * u + g
                pq = bank()
                for kt in range(16):
                    pg.mm(pq, sl[:, g, kt, :], ynT[:, kt, :], start=(kt == 0), stop=(kt == 15))
                t = tmpf[oc % 2]
                pg.tt("dve", t[:, :], pq, gab[:, 8 + oc, :], ALU.mult)
                pg.tt("dve", t1[:, oc, :], t1[:, oc, :], t[:, :], ALU.add)
        pg.dma("sp", xb[:, :, :], xT_v[:, :, t0:t0 + TB], "x")
        for u in range(2):
            sl = w4(wget("wo"), 4, 8)
            for g in range(4):
                oc = 4 * u + g
                pq = bank()
                for kt in range(8):
                    pg.mm(pq, sl[:, g, kt, :], t1[:, kt, :], start=(kt == 0), stop=(kt == 7))
                pg.stt("dve", xb[:, oc, :], pq, modv[:, 16 + oc:17 + oc], xb[:, oc, :], ALU.mult, ALU.add)

        mod_norm(xb, 8, 24, hb)
        for u in range(11):
            sl = w4(wget("wup"), 4, 8)
            for half in range(2):
                j = 2 * u + half
                for gv in range(2):
                    oc = j if gv == 0 else 22 + j
                    pq = bank()
                    for kt in range(8):
                        pg.mm(pq, sl[:, 2 * half + gv, kt, :], hb[:, kt, :], start=(kt == 0), stop=(kt == 7))
                    sb_ = pgs[(2 * j + gv) % 4]
                    pg.copy("pool", sb_[:, 0:2], halo_f[:, oc, :])
                    evac_copy(sb_[:, 2:514], pq)
                    pg.copy("pool", halo_f[:, oc, :], sb_[:, 512:514])
                    pc = bank()
                    for k in range(3):
                        dk = diag(vcol(V_CFW, oc * 3 + k))
                        pg.mm(pc, dk, sb_[:, k:k + 512], start=(k == 0), stop=(k == 2))
                    if gv == 0:
                        fg = fgs[j % 2]
                        pg.act(fg[:, :], pc, AF.Silu, bias=vcol(V_CFB, oc))
                    else:
                        pg.stt("dve", hid[:, j, :], pc, vcol(V_CFB, oc), fg[:, :], ALU.add, ALU.mult)
        for u in range(8):
            sl = w4(wget("wdown"), 1, 22)
            pq = bank()
            for kt in range(22):
                pg.mm(pq, sl[:, 0, kt, :], hid[:, kt, :], start=(kt == 0), stop=(kt == 21))
            pg.stt("dve", xb[:, u, :], pq, modv[:, 40 + u:41 + u], xb[:, u, :], ALU.mult, ALU.add)
        rstd = rms_stats(xb)
        for ft in range(8):
            o = ost[ft % 2]
            pg.stt("dve", o[:, :], xb[:, ft, :], vcol(V_NFG, ft), rstd, ALU.mult, ALU.mult)
            pg.dma("sp", outT_v[:, ft, t0:t0 + TB], o[:, :], "o%d" % (ft % 2))

    pg.emit(final_waits=["o0", "o1"])
    return nc, pg


_CACHE = {}


def kernel(**inputs):
    from concourse.bass_utils import run_bass_kernel_spmd
    sh, per = host_prep(inputs)
    S = per[0]["xT"].shape[1]
    nc, pg = build(S)
    in_maps = []
    for p in per:
        m = dict(sh)
        m.update(p)
        in_maps.append(m)
    n = len(in_maps)
    res = run_bass_kernel_spmd(nc, in_maps, core_ids=list(range(n)))
    out = np.stack([np.ascontiguousarray(r["outT"].T) for r in res.results], axis=0)
    return out.astype(np.float32)
```

```python
import numpy as np
import concourse.bass as bass
import concourse.mybir as mybir

F32 = mybir.dt.float32
BF16 = mybir.dt.bfloat16
AF = mybir.ActivationFunctionType
ALU = mybir.AluOpType
AX = mybir.AxisListType

_ESZ = {F32: 4, BF16: 2}

SB_LO = 16512
SB_HI = 229344


class View:
    __slots__ = ("ap", "space", "base", "reg")

    def __init__(self, ap, space, base):
        self.ap = ap
        self.space = space
        self.base = base
        pat = ap.ap
        S = pat[0][0]
        off = ap.offset
        esz = _ESZ[ap.dtype]
        if S == 0:
            p0, lo = 0, off
        else:
            p0, lo = off // S, off % S
        hi = lo + 1
        for st, cnt in pat[1:]:
            hi += (cnt - 1) * st
        b0 = base + lo * esz
        b1 = base + hi * esz
        if space == "P":
            b0 = (b0 // 2048) * 2048
            b1 = ((b1 + 2047) // 2048) * 2048
            self.reg = ("P", 0, 128, b0, b1)
        else:
            self.reg = ("S", p0, p0 + pat[0][1], b0, b1)

    def __getitem__(self, key):
        return View(self.ap[key], self.space, self.base)

    def f(self, fn):
        return View(fn(self.ap), self.space, self.base)

    def bc(self, axis, n):
        a = self.ap.unsqueeze(axis)
        shp = list(a.shape)
        shp[axis] = n
        return View(a.to_broadcast(shp), self.space, self.base)

    def bitcast(self, dt):
        return View(self.ap.bitcast(dt), self.space, self.base)

    @property
    def shape(self):
        return self.ap.shape


class Buf:
    def __init__(self, prog, name, shape, dtype, space="S"):
        self.name = name
        self.shape = list(shape)
        self.dtype = dtype
        self.space = space
        nc = prog.nc
        free = int(np.prod(shape[1:]))
        nbytes = free * _ESZ[dtype]
        if space == "S":
            off = prog.sb_alloc(nbytes)
            self.base = off
            self.t = nc.alloc_sbuf_tensor_at(name, list(shape), dtype, offset=off)
        else:
            self.base = 0
            self.t = nc.alloc_psum_tensor(name, list(shape), dtype)
        self.nbytes = nbytes

    def __getitem__(self, key):
        return View(self.t[key], self.space, self.base)

    def alias(self, prog, name, shape, dtype, byte_off=0):
        b = Buf.__new__(Buf)
        b.name = name
        b.shape = list(shape)
        b.dtype = dtype
        b.space = "S"
        b.base = self.base + byte_off
        b.nbytes = int(np.prod(shape[1:])) * _ESZ[dtype]
        assert byte_off + b.nbytes <= self.nbytes, (name, byte_off, b.nbytes, self.nbytes)
        b.t = prog.nc.alloc_sbuf_tensor_at(name, list(shape), dtype, offset=b.base)
        return b


class Op:
    __slots__ = ("q", "fn", "deps", "dma_sem", "dma_val", "signal", "count", "idx")


QUEUES = ("pe", "act", "dve", "pool", "sp")


class Prog:
    def __init__(self, nc):
        self.nc = nc
        self.ops = []
        self.sb_ptr = SB_LO
        self.live = {}
        self.dma_counts = {}
        self.dma_sems = {}
        self.n_dma_sems = 0

    def sb_alloc(self, nbytes, align=64):
        off = (self.sb_ptr + align - 1) // align * align
        assert off + nbytes <= SB_HI, ("SBUF overflow", off, nbytes)
        self.sb_ptr = off + nbytes
        return off

    def sbuf(self, name, shape, dtype):
        return Buf(self, name, shape, dtype, "S")

    def psum(self, name, shape, dtype=F32):
        return Buf(self, name, shape, dtype, "P")

    @staticmethod
    def _buckets(reg):
        sp, p0, p1, b0, b1 = reg
        if sp == "P":
            return [("P", k) for k in range(b0 // 2048, (b1 + 2047) // 2048)]
        return [("S", k) for k in range(b0 // 1024, (b1 + 1023) // 1024)]

    @staticmethod
    def _overlap(a, b):
        return a[0] == b[0] and a[1] < b[2] and b[1] < a[2] and a[3] < b[4] and b[3] < a[4]

    @staticmethod
    def _covers(a, b):
        return a[1] <= b[1] and a[2] >= b[2] and a[3] <= b[3] and a[4] >= b[4]

    def _track(self, idx, q, reads, writes):
        deps = {}
        for r in reads:
            for bk in self._buckets(r):
                for rec in self.live.get(bk, ()):
                    if rec[2] and self._overlap(rec[0], r):
                        deps[rec[1]] = True
        for w in writes:
            for bk in self._buckets(w):
                for rec in self.live.get(bk, ()):
                    if self._overlap(rec[0], w):
                        if rec[1] != idx and rec[1] not in deps:
                            deps[rec[1]] = False
        for w in writes:
            for bk in self._buckets(w):
                lst = self.live.setdefault(bk, [])
                lst[:] = [rec for rec in lst if not self._covers(w, rec[0])]
                lst.append([w, idx, True, q])
        for r in reads:
            for bk in self._buckets(r):
                lst = self.live.setdefault(bk, [])
                lst[:] = [rec for rec in lst
                          if not ((not rec[2]) and rec[3] == q and self._covers(r, rec[0]))]
                lst.append([r, idx, False, q])
        deps.pop(idx, None)
        return deps

    def add(self, q, fn, reads=(), writes=(), dma_sem=None):
        op = Op()
        op.q = q
        op.fn = fn
        op.idx = len(self.ops)
        rr = [v.reg for v in reads if isinstance(v, View)]
        ww = [v.reg for v in writes if isinstance(v, View)]
        op.deps = self._track(op.idx, q, rr, ww)
        op.dma_sem = dma_sem
        op.dma_val = None
        if dma_sem is not None:
            self.dma_counts[dma_sem] = self.dma_counts.get(dma_sem, 0) + 16
            op.dma_val = self.dma_counts[dma_sem]
        op.signal = False
        op.count = None
        self.ops.append(op)
        return op

    def _eng(self, e, q):
        return e

    def mm(self, out, lhsT, rhs, start=True, stop=True, **kw):
        return self.add("pe", lambda e: e.matmul(out.ap, lhsT.ap, rhs.ap, start=start, stop=stop, **kw),
                        reads=[lhsT, rhs], writes=[out])

    def transpose(self, out, in_, ident):
        return self.add("pe", lambda e: e.transpose(out.ap, in_.ap, ident.ap),
                        reads=[in_, ident], writes=[out])

    def act(self, out, in_, func, bias=None, scale=None, accum_out=None, q="act"):
        kw = {}
        rd = [in_]
        wr = [out]
        if bias is not None:
            kw["bias"] = bias.ap if isinstance(bias, View) else bias
            rd.append(bias)
        if scale is not None:
            kw["scale"] = scale.ap if isinstance(scale, View) else scale
            rd.append(scale)
        if accum_out is not None:
            kw["accum_out"] = accum_out.ap
            wr.append(accum_out)
        return self.add(q, lambda e: e.activation(out.ap, in_.ap, func, **kw), reads=rd, writes=wr)

    def tt(self, q, out, in0, in1, op):
        return self.add(q, lambda e: e.tensor_tensor(out.ap, in0.ap, in1.ap, op),
                        reads=[in0, in1], writes=[out])

    def ts(self, q, out, in0, s1, op0, s2=None, op1=None, accum_out=None):
        rd = [in0, s1, s2]
        a1 = s1.ap if isinstance(s1, View) else s1
        a2 = s2.ap if isinstance(s2, View) else s2
        kw = {}
        wr = [out]
        if op1 is not None:
            kw["op1"] = op1
        if accum_out is not None:
            kw["accum_out"] = accum_out.ap
            wr.append(accum_out)
        return self.add(q, lambda e: e.tensor_scalar(out.ap, in0.ap, a1, a2, op0, **kw),
                        reads=rd, writes=wr)

    def stt(self, q, out, in0, scalar, in1, op0, op1):
        a = scalar.ap if isinstance(scalar, View) else scalar
        return self.add(q, lambda e: e.scalar_tensor_tensor(out.ap, in0.ap, a, in1.ap, op0, op1),
                        reads=[in0, scalar, in1], writes=[out])

    def copy(self, q, out, in_):
        if q == "act":
            return self.add(q, lambda e: e.copy(out.ap, in_.ap), reads=[in_], writes=[out])
        return self.add(q, lambda e: e.tensor_copy(out.ap, in_.ap), reads=[in_], writes=[out])

    def memset(self, q, out, val):
        return self.add(q, lambda e: e.memset(out.ap, val), reads=[], writes=[out])

    def dma(self, q, out, in_, sem):
        o = out.ap if isinstance(out, View) else out
        i = in_.ap if isinstance(in_, View) else in_
        if sem not in self.dma_sems:
            self.dma_sems[sem] = None
        return self.add(q, lambda e: e.dma_start(out=o, in_=i),
                        reads=[in_] if isinstance(in_, View) else [],
                        writes=[out] if isinstance(out, View) else [],
                        dma_sem=sem)

    def emit(self, final_waits=()):
        nc = self.nc
        ops = self.ops
        need = []
        for op in ops:
            best = {}
            for j, is_raw in op.deps.items():
                pj = ops[j]
                if pj.dma_sem is not None:
                    key = ("d", pj.dma_sem)
                    if key not in best or best[key].idx < pj.idx:
                        best[key] = pj
                    continue
                if pj.q == op.q:
                    if op.q == "pe":
                        continue
                key = ("c", pj.q)
                if key not in best or best[key].idx < pj.idx:
                    best[key] = pj
            for key, pj in best.items():
                if key[0] == "c":
                    pj.signal = True
            need.append(best)
        cnt = {q: 0 for q in QUEUES}
        for op in ops:
            if op.signal:
                cnt[op.q] += 1
                op.count = cnt[op.q]
        self.sig_counts = dict(cnt)
        from contextlib import ExitStack
        with ExitStack() as es:
            csem = {q: es.enter_context(nc.semaphore("c_" + q)) for q in QUEUES}
            dsem = {s: es.enter_context(nc.semaphore("d_" + s)) for s in self.dma_sems}
            block = es.enter_context(nc.Block())
            per_q = {q: [] for q in QUEUES}
            for op, nd in zip(ops, need):
                per_q[op.q].append((op, nd))

            def run_queue(q, eng):
                waited = {}
                for op, nd in per_q[q]:
                    for key, pj in nd.items():
                        if key[0] == "d":
                            sem, val = dsem[key[1]], pj.dma_val
                        else:
                            sem, val = csem[key[1]], pj.count
                        if waited.get(key, 0) >= val:
                            continue
                        waited[key] = val
                        eng.wait_ge(sem, val)
                    ins = op.fn(eng)
                    if op.dma_sem is not None:
                        ins.then_inc(dsem[op.dma_sem], 16)
                    elif op.signal:
                        ins.then_inc(csem[q], 1)
                if q == "sp":
                    for s in final_waits:
                        eng.wait_ge(dsem[s], self.dma_counts[s])

            @block.tensor
            def _(e):
                run_queue("pe", e)

            @block.scalar
            def _(e):
                run_queue("act", e)

            @block.vector
            def _(e):
                run_queue("dve", e)

            @block.gpsimd
            def _(e):
                run_queue("pool", e)

            @block.sync
            def _(e):
                run_queue("sp", e)


D = 1024
S_FULL = 2048
TB = 512
KB = 1024
EPS = 1e-6
EPS_SSM = 1e-5

V_CT, V_BADA, V_N1G, V_CAB, V_LNG, V_LNB, V_BAO, V_BG, V_N2G, V_CFB, V_NFG, V_CBCB = \
    0, 8, 56, 64, 72, 80, 88, 96, 112, 120, 164, 172
V_CAW = 180
V_CSW = V_CAW + 248
V_CFW = V_CSW + 96
V_CSB = V_CFW + 132
V_CFW2 = V_CSB + 24
NV = V_CFW2 + 132

C_ID, C_L2, C_TRT, C_BON, C_OTOP, C_OBOT, C_TRI2 = 0, 128, 256, 384, 512, 640, 768
NCST = 768 + 64


def _units_fm(W, starts, G, KT):
    nu = len(starts) // G
    out = np.empty((nu, 128, G, KT, 128), np.float32)
    for i, s in enumerate(starts):
        blk = W[:, s:s + 128].reshape(KT, 128, 128)
        out[i // G, :, i % G] = blk.transpose(1, 0, 2)
    return out.reshape(nu, 128, G * KT * 128)


def _fm_vec(v):
    return np.ascontiguousarray(v.reshape(-1, 128).T)


def host_prep(inp):
    f = lambda k: np.asarray(inp[k], np.float32)
    w_in = f("w_in")[0]
    sh = {}
    wada = f("w_ada")[0]
    sh["wada"] = _units_fm(wada, [i * 128 for i in range(48)], 4, 8)
    ag_starts = []
    for c in range(8):
        ag_starts += [1024 + c * 128, c * 128]
    sh["wag"] = _units_fm(w_in, ag_starts, 4, 8)
    sh["wxs"] = _units_fm(w_in, [4096 + i * 128 for i in range(16)], 4, 8)
    sh["wbc"] = _units_fm(w_in, [6144 + i * 128 for i in range(8)], 4, 8)
    wz = w_in[:, 2048:4096].reshape(8, 128, 4, 512)
    sh["wz"] = np.ascontiguousarray(wz.transpose(2, 1, 0, 3)).reshape(4, 128, 4096)
    sh["wdt"] = np.ascontiguousarray(w_in[:, 7168:7200].reshape(8, 128, 32).transpose(1, 0, 2)).reshape(128, 256)
    sh["wgate"] = _units_fm(f("w_gate")[0], [i * 128 for i in range(16)], 4, 8)
    sh["waout"] = _units_fm(f("w_a_out")[0], [i * 128 for i in range(8)], 4, 8)
    sh["wbout"] = _units_fm(f("w_b_out")[0], [i * 128 for i in range(8)], 2, 16)
    sh["wo"] = _units_fm(f("w_o")[0], [i * 128 for i in range(8)], 4, 8)
    up_starts = []
    for j in range(22):
        up_starts += [j * 128, 2816 + j * 128]
    sh["wup"] = _units_fm(f("w_up")[0], up_starts, 4, 8)
    sh["wdown"] = _units_fm(f("w_down")[0], [i * 128 for i in range(8)], 1, 22)
    vec = np.zeros((128, NV), np.float32)
    vec[:, V_BADA:V_BADA + 48] = _fm_vec(f("b_ada")[0])
    vec[:, V_N1G:V_N1G + 8] = _fm_vec(f("norm1_g")[0])
    vec[:, V_CAB:V_CAB + 8] = _fm_vec(f("conv_a_b")[0])
    vec[:, V_LNG:V_LNG + 8] = _fm_vec(f("ln_a_g")[0])
    vec[:, V_LNB:V_LNB + 8] = _fm_vec(f("ln_a_b")[0])
    vec[:, V_BAO:V_BAO + 8] = _fm_vec(f("b_a_out")[0])
    vec[:, V_BG:V_BG + 16] = _fm_vec(f("b_gate")[0])
    vec[:, V_N2G:V_N2G + 8] = _fm_vec(f("norm2_g")[0])
    vec[:, V_CFB:V_CFB + 44] = _fm_vec(f("conv_ffn_b")[0])
    vec[:, V_NFG:V_NFG + 8] = _fm_vec(f("norm_f_g"))
    csb = f("conv_ssm_b")[0]
    vec[:, V_CSB:V_CSB + 24] = _fm_vec(csb)
    caw = f("conv_a_w")[0]
    vec[:, V_CAW:V_CAW + 248] = caw.T.reshape(8, 128, 31).transpose(1, 0, 2).reshape(128, 248)
    csw = f("conv_ssm_w")[0]
    vec[:, V_CSW:V_CSW + 96] = csw.T.reshape(24, 128, 4).transpose(1, 0, 2).reshape(128, 96)
    cfw = f("conv_ffn_w")[0]
    cfw_l = cfw.T.reshape(44, 128, 3).transpose(1, 0, 2)
    vec[:, V_CFW:V_CFW + 132] = cfw_l.reshape(128, 132)
    order = []
    for j in range(22):
        order += [j, 22 + j]
    vec[:, V_CFW2:V_CFW2 + 132] = cfw_l[:, order, :].reshape(128, 132)
    tmc = np.zeros((128, 96), np.float32)
    tmc[:, 0:32] = f("dt_bias")[0][None, :]
    tmc[:, 32:64] = f("a_log")[0][None, :]
    tmc[:, 64:96] = f("d_skip")[0][None, :]
    sh["tmc"] = tmc
    sh["gnb"] = np.ascontiguousarray(np.broadcast_to(f("ssm_norm_g")[0][None, :], (128, 2048)))
    cst = np.zeros((128, NCST), np.float32)
    idx = np.arange(128)
    same = (idx[:, None] // 64) == (idx[None, :] // 64)
    cst[:, C_ID:C_ID + 128] = np.eye(128)
    cst[:, C_L2:C_L2 + 128] = same & (idx[:, None] > idx[None, :])
    cst[:, C_TRT:C_TRT + 128] = same & (idx[:, None] <= idx[None, :])
    cst[:, C_BON:C_BON + 128] = same
    cst[:, C_OTOP:C_OTOP + 128] = (idx[:, None] < 64)
    cst[:, C_OBOT:C_OBOT + 128] = (idx[:, None] >= 64)
    cst[:, C_TRI2:C_TRI2 + 64] = ((idx[:, None] % 64) <= np.arange(64)[None, :])
    sh["cst"] = cst
    x = f("x")
    c = f("c")
    per = []
    for b in range(x.shape[0]):
        v = vec.copy()
        v[:, V_CT:V_CT + 8] = _fm_vec(c[b])
        per.append({"xT": np.ascontiguousarray(x[b].T), "vec": v})
    return sh, per


def build(S):
    NBLK = S // TB
    nc = bass.Bass("TRN2", target_bir_lowering=False)
    pg = Prog(nc)

    def din(name, shape):
        return nc.dram_tensor(name, list(shape), F32, kind="ExternalInput").ap()

    xT = din("xT", [D, S])
    vec_d = din("vec", [128, NV])
    cst_d = din("cst", [128, NCST])
    tmc_d = din("tmc", [128, 96])
    gnb_d = din("gnb", [128, 2048])
    wdt_d = din("wdt", [128, 256])
    wd = {k: din(k, shp) for k, shp in [
        ("wada", [12, 128, 4096]), ("wag", [4, 128, 4096]), ("wxs", [4, 128, 4096]),
        ("wbc", [2, 128, 4096]), ("wz", [4, 128, 4096]), ("wgate", [4, 128, 4096]),
        ("waout", [2, 128, 4096]), ("wbout", [4, 128, 4096]), ("wo", [2, 128, 4096]),
        ("wup", [11, 128, 4096]), ("wdown", [8, 128, 2816])]}
    outT = nc.dram_tensor("outT", [D, S], F32, kind="ExternalOutput").ap()

    base = SB_LO

    def at(name, shape, dtype, off):
        b = Buf.__new__(Buf)
        b.name, b.shape, b.dtype, b.space = name, list(shape), dtype, "S"
        b.base = base + off
        b.nbytes = int(np.prod(shape[1:])) * _ESZ[dtype]
        assert b.base % 32 == 0 and b.base + b.nbytes <= SB_HI, (name, off, b.nbytes)
        b.t = nc.alloc_sbuf_tensor_at(name, list(shape), dtype, offset=b.base)
        b.end = off + b.nbytes
        return b

    class Seq:
        def __init__(self, start, limit):
            self.p, self.limit = start, limit

        def __call__(self, name, shape, dtype):
            off = (self.p + 63) // 64 * 64
            b = at(name, shape, dtype, off)
            self.p = b.end
            assert self.p <= self.limit, ("region overflow", name, self.p, self.limit)
            return b

    P = Seq(0, 68 * KB)
    cst = P("cst", [128, NCST], F32)
    vec = P("vec", [128, NV], F32)
    tmc = P("tmc", [128, 96], F32)
    vecb = P("vecb", [128, NV], BF16)
    idb = P("idb", [128, 128], BF16)
    onb = P("onb", [128, 128], BF16)
    onf = P("onf", [128, 128], F32)
    modv = P("modv", [128, 48], F32)
    gs = P("gs", [128, 16], F32)
    abc = P("abc", [128, 32], F32)
    dbc = P("dbc", [128, 32], BF16)
    scb = P("scb", [128, 8], BF16)
    wdt = P("wdt", [128, 8, 32], BF16)
    gnb = P("gnb", [128, 2048], BF16)
    halo_u = P("halo_u", [128, 8, 30], BF16)
    halo_x = P("halo_x", [128, 24, 3], BF16)
    halo_f = P("halo_f", [128, 44, 2], BF16)
    Sst = P("Sst", [128, 4, 512], F32)
    SbA = [P("SbA0", [128, 4, 512], BF16), P("SbA1", [128, 4, 512], BF16)]
    SbB = P("SbB", [128, 4, 512], BF16)
    NSLOT = 3
    wslots = [P("wsl%d" % i, [128, 4096], BF16) for i in range(NSLOT)]
    hb = P("hb", [128, 8, 512], BF16)
    assert P.p <= 68 * KB, P.p
    M = Seq(68 * KB, 144 * KB)
    sz = M("sz", [128, 4, 2048], BF16)
    xsT = M("xsT", [128, 4, 2048], BF16)
    BT = M("BT", [128, 4, 512], BF16)
    BfT = M("BfT", [128, 4, 512], BF16)
    CfT = M("CfT", [128, 4, 512], BF16)
    C0 = M("C0", [128, 4, 4, 128], BF16)
    C1 = M("C1", [128, 4, 4, 128], BF16)
    ua = M("ua", [128, 8, 512], F32)
    ynT = at("ynT", [128, 16, 512], BF16, ua.base - base)
    uA = M("uA", [128, 8, 512], BF16)
    hid = at("hid", [128, 22, 512], BF16, 68 * KB)
    A0 = 144 * KB
    A_END = SB_HI - base
    xb = at("xb", [128, 8, 512], F32, A_END - 16 * KB - 64)
    A_LIM = A_END - 16 * KB - 64
    G_ = Seq(A0, A_LIM)
    sq = G_("sq", [128, 8, 512], BF16)
    st0 = G_("st0", [128, 512], F32)
    st1 = G_("st1", [128, 512], F32)
    st2 = G_("st2", [128, 512], F32)
    tmpf = [G_("tmpf0", [128, 512], F32), G_("tmpf1", [128, 512], F32)]
    DB = 16
    dring = [G_("dgr%d" % i, [128, DB, 128], BF16) for i in range(2)]
    g_end = G_.p
    Q = Seq(g_end, A_END)
    ub = Q("ub", [128, 8, 542], BF16)
    xbcp = Q("xbcp", [128, 24, 515], BF16)
    sgs = [Q("sg0", [128, 512], BF16), Q("sg1", [128, 512], BF16)]
    gab = at("gab", [128, 16, 512], BF16, sz.base - base)
    t1 = at("t1", [128, 8, 512], BF16, xsT.base - base)
    Q3 = Seq(g_end, A_LIM)
    pgs = [Q3("pgs%d" % i, [128, 514], BF16) for i in range(4)]
    fgs = [Q3("fg%d" % i, [128, 512], BF16) for i in range(2)]
    ost = [Q3("ost%d" % i, [128, 512], F32) for i in range(2)]
    R = Seq(A0, A_END)
    R2 = R("R2", [128, 32, 64], F32)
    eb = R("eb", [128, 32, 64], BF16)
    m2 = R("m2", [128, 32, 128], BF16)
    xdt = R("xdt", [128, 32, 64], BF16)
    xw = R("xw", [128, 32, 64], BF16)
    xsD = R("xsD", [128, 32, 64], BF16)
    tyo = R("tyo", [128, 4, 512], F32)
    sqf = R("sqf", [128, 4, 512], F32)
    yn = R("yn", [128, 2048], BF16)
    CBm = R("CBm", [128, 4, 128], BF16)
    CBc = R("CBc", [128, 4, 64], BF16)
    sm = R("sm", [128, 16, 32], F32)

    ps = pg.psum("ps", [128, 4096], F32)
    bank_ctr = [0]

    def bank():
        i = bank_ctr[0] % 3
        bank_ctr[0] += 1
        return ps[:, 512 * i:512 * (i + 1)]

    def fixed_bank(i):
        return ps[:, 512 * i:512 * (i + 1)]

    ident_f = cst[:, C_ID:C_ID + 128]
    L2 = cst[:, C_L2:C_L2 + 128]
    trT = cst[:, C_TRT:C_TRT + 128]
    bones = cst[:, C_BON:C_BON + 128]
    otop = cst[:, C_OTOP:C_OTOP + 128]
    obot = cst[:, C_OBOT:C_OBOT + 128]
    tri2 = cst[:, C_TRI2:C_TRI2 + 64]

    def vcol(c0, i=0):
        return vec[:, c0 + i:c0 + i + 1]

    sched = [("wada", u) for u in range(12)]
    blk_sched = ([("wag", u) for u in range(4)] + [("wxs", u) for u in range(4)] +
                 [("wbc", u) for u in range(2)] + [("wz", u) for u in range(4)] +
                 [("wgate", u) for u in range(4)] + [("waout", u) for u in range(2)] +
                 [("wbout", u) for u in range(4)] + [("wo", u) for u in range(2)] +
                 [("wup", u) for u in range(11)] + [("wdown", u) for u in range(8)])
    for _ in range(NBLK):
        sched += blk_sched
    wstate = {"issued": 0, "used": 0}

    def w_issue_upto(n):
        while wstate["issued"] < min(n, len(sched)):
            i = wstate["issued"]
            name, u = sched[i]
            ncol = 2816 if name == "wdown" else 4096
            slot = wslots[i % NSLOT]
            pg.dma("pool", slot[:, 0:ncol], wd[name][u], "w%d" % (i % NSLOT))
            wstate["issued"] += 1

    def wget(name):
        i = wstate["used"]
        assert sched[i][0] == name, (sched[i], name)
        w_issue_upto(i + NSLOT)
        wstate["used"] += 1
        return wslots[i % NSLOT]

    def w4(slot, G, KT):
        return slot[:, 0:G * KT * 128].f(lambda a: a.rearrange("p (g k m) -> p g k m", g=G, k=KT))

    pg.dma("sp", cst[:, :], cst_d, "c0")
    pg.dma("sp", vec[:, :], vec_d, "c1")
    pg.dma("sp", tmc[:, :], tmc_d, "c2")
    pg.dma("pool", gnb[:, :], gnb_d, "c3")
    pg.dma("pool", wdt[:, :, :].f(lambda a: a.rearrange("p k n -> p (k n)")), wdt_d, "c4")
    w_issue_upto(NSLOT)
    pg.copy("dve", idb[:, :], ident_f)
    pg.copy("dve", vecb[:, :], vec[:, :])
    pg.memset("dve", onb[:, :], 1.0)
    pg.memset("dve", onf[:, :], 1.0)
    pg.memset("pool", halo_u[:, :, :], 0.0)
    pg.memset("pool", halo_x[:, :, :], 0.0)
    pg.memset("pool", halo_f[:, :, :], 0.0)
    pg.memset("pool", Sst[:, :, :], 0.0)
    pg.memset("pool", SbA[0][:, :, :], 0.0)
    pg.memset("pool", C0[:, :, :, :], 0.0)
    pg.memset("pool", C1[:, :, :, :], 0.0)
    pg.act(abc[:, :], tmc[:, 32:64], AF.Exp)
    pg.ts("dve", abc[:, :], abc[:, :], -1.0, ALU.mult)
    pg.copy("dve", dbc[:, :], tmc[:, 64:96])
    pg.act(scb[:, :], vec[:, V_CT:V_CT + 8], AF.Silu)
    psm = fixed_bank(3)
    for u in range(12):
        sl = w4(wget("wada"), 4, 8)
        for g in range(4):
            oc = 4 * u + g
            for kt in range(8):
                pg.mm(psm[:, oc:oc + 1], sl[:, g, kt, :], scb[:, kt:kt + 1], start=(kt == 0), stop=(kt == 7))
    pg.tt("dve", modv[:, :], psm[:, 0:48], vec[:, V_BADA:V_BADA + 48], ALU.add)
    pg.stt("dve", gs[:, 0:8], modv[:, 8:16], 1.0, vec[:, V_N1G:V_N1G + 8], ALU.add, ALU.mult)
    pg.stt("dve", gs[:, 8:16], modv[:, 32:40], 1.0, vec[:, V_N2G:V_N2G + 8], ALU.add, ALU.mult)

    evac_ctr = [0]

    def evac_copy(out, in_):
        q = "act" if evac_ctr[0] % 2 == 0 else "dve"
        evac_ctr[0] += 1
        pg.copy(q, out, in_)

    dg_ctr = [0]

    def diag_batch(c0, n):
        r = dring[dg_ctr[0] % 2]
        dg_ctr[0] += 1
        pg.tt("dve", r[:, 0:n, :], idb[:, :].bc(1, n), vecb[:, c0:c0 + n].bc(2, 128), ALU.mult)
        return [r[:, i, :] for i in range(n)]

    def rms_stats(xbuf):
        pst = fixed_bank(3)
        for ft in range(8):
            pg.act(sq[:, ft, :], xbuf[:, ft, :], AF.Square)
        for ft in range(8):
            pg.mm(pst, onb[:, :], sq[:, ft, :], start=(ft == 0), stop=(ft == 7))
        pg.act(st0[:, :], pst, AF.Sqrt, bias=EPS, scale=1.0 / D)
        pg.add("dve", lambda e: e.reciprocal(st1[:, :].ap, st0[:, :].ap), reads=[st0[:, :]], writes=[st1[:, :]])
        return st1[:, :]

    def mod_norm(xbuf, gcol0, shcol0, out):
        rstd = rms_stats(xbuf)
        for ft in range(8):
            t = tmpf[ft % 2]
            pg.tt("dve", t[:, :], xbuf[:, ft, :], rstd, ALU.mult)
            pg.act(out[:, ft, :], t[:, :], AF.Identity, bias=modv[:, shcol0 + ft:shcol0 + ft + 1],
                   scale=gs[:, gcol0 + ft:gcol0 + ft + 1])

    xT_v = xT.rearrange("(f p) s -> p f s", p=128)
    outT_v = outT.rearrange("(f p) s -> p f s", p=128)

    for blk in range(NBLK):
        t0 = blk * TB
        pg.dma("sp", xb[:, :, :], xT_v[:, :, t0:t0 + TB], "x")
        mod_norm(xb, 0, 0, hb)
        pg.copy("pool", ub[:, :, 0:30], halo_u[:, :, :])
        pg.copy("pool", xbcp[:, :, 0:3], halo_x[:, :, :])
        for u in range(4):
            sl = w4(wget("wag"), 4, 8)
            for half in range(2):
                c = 2 * u + half
                pa = bank()
                for kt in range(8):
                    pg.mm(pa, sl[:, 2 * half, kt, :], hb[:, kt, :], start=(kt == 0), stop=(kt == 7))
                sg = sgs[c % 2]
                pg.act(sg[:, :], pa, AF.Sigmoid)
                pv = bank()
                for kt in range(8):
                    pg.mm(pv, sl[:, 2 * half + 1, kt, :], hb[:, kt, :], start=(kt == 0), stop=(kt == 7))
                pg.tt("dve", ub[:, c, 30:542], pv, sg[:, :], ALU.mult)
        for (nm, nu, ct0) in (("wxs", 4, 0), ("wbc", 2, 16)):
            for u in range(nu):
                sl = w4(wget(nm), 4, 8)
                for g in range(4):
                    ct = ct0 + 4 * u + g
                    pp = bank()
                    for kt in range(8):
                        pg.mm(pp, sl[:, g, kt, :], hb[:, kt, :], start=(kt == 0), stop=(kt == 7))
                    evac_copy(xbcp[:, ct, 3:515], pp)
        for u in range(4):
            sl = wget("wz")[:, :].f(lambda a: a.rearrange("p (k n) -> p k n", k=8))
            for pr in range(4):
                pz = bank()
                for kt in range(8):
                    pg.mm(pz, hb[:, kt, pr * 128:(pr + 1) * 128], sl[:, kt, :], start=(kt == 0), stop=(kt == 7))
                pg.act(sz[:, pr, u * 512:(u + 1) * 512], pz, AF.Silu)
        for cg in range(6):
            dw = diag_batch(V_CSW + cg * 16, 16)
            dlist = [[dw[4 * j + k] for k in range(4)] for j in range(4)]
            if cg < 5:
                dbs = diag_batch(V_CSB + cg * 4, 4)
                for j in range(4):
                    dlist[j].append(dbs[j])
            if cg < 5:
                dst = xsT if cg < 4 else BT
                for pr in range(4):
                    pt = bank()
                    for j in range(4):
                        ct = 4 * cg + j
                        o = pt[:, j * 128:(j + 1) * 128]
                        for k in range(4):
                            pg.mm(o, xbcp[:, ct, pr * 128 + k:pr * 128 + k + 128], dlist[j][k],
                                  start=(k == 0), stop=False)
                        pg.mm(o, onb[:, :], dlist[j][4], start=False, stop=True)
                    if cg < 4:
                        pg.act(xsT[:, pr, cg * 512:(cg + 1) * 512], pt, AF.Silu)
                    else:
                        pg.act(BT[:, pr, :], pt, AF.Silu)
            if cg >= 4:
                dstf = BfT if cg == 4 else CfT
                for j in range(4):
                    ct = 4 * cg + j
                    pf = bank()
                    for k in range(4):
                        pg.mm(pf, dlist[j][k], xbcp[:, ct, k:k + 512], start=(k == 0), stop=(k == 3))
                    pg.act(dstf[:, j, :], pf, AF.Silu, bias=vcol(V_CSB, ct))
                    if cg == 5:
                        cv = CfT[:, j, :].f(lambda a: a.rearrange("p (r c l) -> p r c l", r=4, c=2))
                        pg.copy("pool", C0[:, j, :, 0:64], cv[:, :, 0, :])
                        pg.copy("pool", C1[:, j, :, 64:128], cv[:, :, 1, :])
        pg.copy("pool", halo_x[:, :, :], xbcp[:, :, 512:515])
        ps1 = fixed_bank(3)
        for ct in range(8):
            pc = bank()
            for k0 in (0, 16):
                nk = min(16, 31 - k0)
                dks = diag_batch(V_CAW + ct * 31 + k0, nk)
                for kk in range(nk):
                    k = k0 + kk
                    pg.mm(pc, dks[kk], ub[:, ct, k:k + 512], start=(k == 0), stop=(k == 30))
            pg.act(ua[:, ct, :], pc, AF.Identity, bias=vcol(V_CAB, ct))
            pg.act(sq[:, ct, :], pc, AF.Square, bias=vcol(V_CAB, ct))
        pg.copy("pool", halo_u[:, :, :], ub[:, :, 512:542])
        for ct in range(8):
            pg.mm(ps1, onf[:, :], ua[:, ct, :], start=(ct == 0), stop=(ct == 7))
        ps2 = bank()
        for ct in range(8):
            pg.mm(ps2, onb[:, :], sq[:, ct, :], start=(ct == 0), stop=(ct == 7))
        pg.ts("dve", st0[:, :], ps1, 1.0 / D, ALU.mult)
        pg.tt("dve", st2[:, :], st0[:, :], st0[:, :], ALU.mult)
        pg.stt("dve", st2[:, :], ps2, 1.0 / D, st2[:, :], ALU.mult, ALU.subtract)
        pg.act(st2[:, :], st2[:, :], AF.Sqrt, bias=EPS)
        pg.add("dve", lambda e: e.reciprocal(st1[:, :].ap, st2[:, :].ap), reads=[st2[:, :]], writes=[st1[:, :]])
        for ct in range(8):
            t = tmpf[ct % 2]
            pg.tt("dve", t[:, :], ua[:, ct, :], st0[:, :], ALU.subtract)
            pg.tt("dve", t[:, :], t[:, :], st1[:, :], ALU.mult)
            pg.act(uA[:, ct, :], t[:, :], AF.Silu, bias=vcol(V_LNB, ct), scale=vcol(V_LNG, ct))

        pg.memset("pool", m2[:, :, :], 0.0)
        for pr in range(4):
            gp = blk * 4 + pr
            tp = pr * 128
            SA_in = SbA[gp % 2]
            SA_out = SbA[(gp + 1) % 2]
            pm = fixed_bank(3)
            for kt in range(8):
                pg.mm(pm[:, 0:32], hb[:, kt, tp:tp + 128], wdt[:, kt, :], start=(kt == 0), stop=(kt == 7))
            s_xb, s_ex, s_dt, s_da, s_eA, s_d2, s_A, s_w, s_cd0, s_cd1, s_ss, s_rs = [sm[:, i, :] for i in range(12)]
            pg.tt("dve", s_xb, pm[:, 0:32], tmc[:, 0:32], ALU.add)
            pg.act(s_ex, s_xb, AF.Exp)
            pg.act(s_dt, s_ex, AF.Ln, bias=1.0)
            pg.tt("dve", s_da, s_dt, abc[:, :], ALU.mult)
            pg.mm(pm[:, 32:64], trT, s_da)
            pg.mm(pm[:, 64:96], bones, s_da)
            pg.mm(pm[:, 96:128], otop, s_da)
            pg.mm(pm[:, 128:160], obot, s_da)
            pg.act(s_eA, pm[:, 32:64], AF.Exp)
            pg.copy("act", s_A, pm[:, 32:64])
            pg.tt("dve", s_d2, pm[:, 64:96], s_A, ALU.subtract)
            pg.act(s_d2, s_d2, AF.Exp)
            pg.tt("dve", s_w, s_dt, s_d2, ALU.mult)
            pg.act(s_cd0, pm[:, 96:128], AF.Exp)
            pg.act(s_cd1, pm[:, 128:160], AF.Exp)
            xs3 = xsT[:, pr, :].f(lambda a: a.rearrange("p (h d) -> p h d", h=32))
            pg.tt("dve", xdt[:, :, :], xs3, s_dt.bc(2, 64), ALU.mult)
            pg.tt("dve", xw[:, :, :], xs3, s_w.bc(2, 64), ALU.mult)
            pg.tt("dve", xsD[:, :, :], xs3, dbc[:, :].bc(2, 64), ALU.mult)
            for j in range(2):
                cdj = s_cd0 if j == 0 else s_cd1
                Sb_out = SbB if j == 0 else SA_out
                for g in range(4):
                    pd = bank()
                    pg.mm(pd, BT[64 * j:64 * j + 64, pr, g * 128:(g + 1) * 128],
                          xw[64 * j:64 * j + 64, g * 8:(g + 1) * 8, :].f(lambda a: a.rearrange("p h d -> p (h d)")))
                    S3 = Sst[:, g, :].f(lambda a: a.rearrange("p (h d) -> p h d", h=8))
                    pg.tt("pool", S3, S3, cdj[:, g * 8:(g + 1) * 8].bc(2, 64), ALU.mult)
                    pg.tt("dve", Sst[:, g, :], Sst[:, g, :], pd, ALU.add)
                    pg.copy("act", Sb_out[:, g, :], Sst[:, g, :])
            pg.tt("dve", R2[:, :, :], tri2.bc(1, 32), s_da.bc(2, 64), ALU.mult)
            R2f = R2[:, :, :].f(lambda a: a.rearrange("p h l -> p (h l)"))
            ebf = eb[:, :, :].f(lambda a: a.rearrange("p h l -> p (h l)"))
            for q in range(4):
                pg.mm(fixed_bank(4 + q), L2, R2f[:, q * 512:(q + 1) * 512])
            for q in range(4):
                pg.act(ebf[:, q * 512:(q + 1) * 512], fixed_bank(4 + q), AF.Exp)
            pcb = bank()
            for g in range(4):
                pg.mm(pcb[:, g * 128:(g + 1) * 128], BfT[:, g, tp:tp + 128], CfT[:, g, tp:tp + 128])
            pg.tt("dve", CBm[:, :, :], pcb.f(lambda a: a.rearrange("p (g l) -> p g l", g=4)), trT.bc(1, 4), ALU.mult)
            pg.tt("dve", CBc[:, :, :], CBm[:, :, 0:64], CBm[:, :, 64:128], ALU.add)
            for g in range(4):
                for j in range(2):
                    pg.tt("dve", m2[64 * j:64 * j + 64, g * 8:(g + 1) * 8, 64 * j:64 * j + 64],
                          eb[64 * j:64 * j + 64, g * 8:(g + 1) * 8, :],
                          CBc[64 * j:64 * j + 64, g, :].bc(1, 8), ALU.mult)
            for h in range(32):
                g = h // 8
                o = fixed_bank(4 + g)[:, (h % 8) * 64:(h % 8) * 64 + 64]
                pg.mm(o, m2[:, h, :], xdt[:, h, :], start=True, stop=False)
                pg.mm(o, idb[:, :], xsD[:, h, :], start=False, stop=True)
            for g in range(4):
                po = bank()
                pg.mm(po, C0[:, g, pr, :], SA_in[:, g, :], start=True, stop=False)
                pg.mm(po, C1[:, g, pr, :], SbB[:, g, :], start=False, stop=True)
                ty3 = tyo[:, g, :].f(lambda a: a.rearrange("p (h d) -> p h d", h=8))
                pg.tt("dve", ty3, po.f(lambda a: a.rearrange("p (h d) -> p h d", h=8)),
                      s_eA[:, g * 8:(g + 1) * 8].bc(2, 64), ALU.mult)
                pg.tt("dve", tyo[:, g, :], tyo[:, g, :], fixed_bank(4 + g), ALU.add)
                pg.tt("dve", tyo[:, g, :], tyo[:, g, :], sz[:, pr, g * 512:(g + 1) * 512], ALU.mult)
                pg.act(sqf[:, g, :], tyo[:, g, :], AF.Square)
            pg.add("dve", lambda e: e.tensor_reduce(s_ss[:, 0:4].ap, sqf[:, :, :].ap, AX.X, ALU.add),
                   reads=[sqf[:, :, :]], writes=[s_ss[:, 0:4]])
            pg.act(s_ss[:, 0:4], s_ss[:, 0:4], AF.Sqrt, bias=EPS_SSM, scale=1.0 / 512)
            pg.add("dve", lambda e: e.reciprocal(s_rs[:, 0:4].ap, s_ss[:, 0:4].ap),
                   reads=[s_ss[:, 0:4]], writes=[s_rs[:, 0:4]])
            for g in range(4):
                pg.stt("dve", yn[:, g * 512:(g + 1) * 512], tyo[:, g, :], s_rs[:, g:g + 1],
                       gnb[:, g * 512:(g + 1) * 512], ALU.mult, ALU.mult)
            for c4 in range(4):
                ptb = bank().bitcast(BF16)
                for j in range(4):
                    ct = 4 * c4 + j
                    pg.transpose(ptb[:, j * 128:(j + 1) * 128], yn[:, ct * 128:(ct + 1) * 128], idb[:, :])
                evac_copy(ynT[:, 4 * c4:4 * c4 + 4, tp:tp + 128],
                          ptb[:, 0:512].f(lambda a: a.rearrange("p (c t) -> p c t", c=4)))

        for u in range(4):
            sl = w4(wget("wgate"), 4, 8)
            for g in range(4):
                oc = 4 * u + g
                pq = bank()
                for kt in range(8):
                    pg.mm(pq, sl[:, g, kt, :], hb[:, kt, :], start=(kt == 0), stop=(kt == 7))
                pg.act(gab[:, oc, :], pq, AF.Sigmoid, bias=vcol(V_BG, oc))
        for u in range(2):
            sl = w4(wget("waout"), 4, 8)
            for g in range(4):
                oc = 4 * u + g
                pq = bank()
                for kt in range(8):
                    pg.mm(pq, sl[:, g, kt, :], uA[:, kt, :], start=(kt == 0), stop=(kt == 7))
                pg.stt("dve", t1[:, oc, :], pq, vcol(V_BAO, oc), gab[:, oc, :], ALU.add, ALU.mult)
        for u in range(4):
            sl = w4(wget("wbout"), 2, 16)
            for g in range(2):
                oc = 2 * u + g
                pq = bank()
                for kt in range(16):
                    pg.mm(pq, sl[:, g, kt, :], ynT[:, kt, :], start=(kt == 0), stop=(kt == 15))
                t = tmpf[oc % 2]
                pg.tt("dve", t[:, :], pq, gab[:, 8 + oc, :], ALU.mult)
                pg.tt("dve", t1[:, oc, :], t1[:, oc, :], t[:, :], ALU.add)
        pg.dma("sp", xb[:, :, :], xT_v[:, :, t0:t0 + TB], "x")
        for u in range(2):
            sl = w4(wget("wo"), 4, 8)
            for g in range(4):
                oc = 4 * u + g
                pq = bank()
                for kt in range(8):
                    pg.mm(pq, sl[:, g, kt, :], t1[:, kt, :], start=(kt == 0), stop=(kt == 7))
                pg.stt("dve", xb[:, oc, :], pq, modv[:, 16 + oc:17 + oc], xb[:, oc, :], ALU.mult, ALU.add)

        mod_norm(xb, 8, 24, hb)
        for u in range(11):
            sl = w4(wget("wup"), 4, 8)
            dfs = diag_batch(V_CFW2 + u * 12, 12)
            for half in range(2):
                j = 2 * u + half
                for gv in range(2):
                    oc = j if gv == 0 else 22 + j
                    pq = bank()
                    for kt in range(8):
                        pg.mm(pq, sl[:, 2 * half + gv, kt, :], hb[:, kt, :], start=(kt == 0), stop=(kt == 7))
                    sb_ = pgs[(2 * j + gv) % 4]
                    pg.copy("pool", sb_[:, 0:2], halo_f[:, oc, :])
                    evac_copy(sb_[:, 2:514], pq)
                    pg.copy("pool", halo_f[:, oc, :], sb_[:, 512:514])
                    pc = bank()
                    for k in range(3):
                        dk = dfs[(2 * half + gv) * 3 + k]
                        pg.mm(pc, dk, sb_[:, k:k + 512], start=(k == 0), stop=(k == 2))
                    if gv == 0:
                        fg = fgs[j % 2]
                        pg.act(fg[:, :], pc, AF.Silu, bias=vcol(V_CFB, oc))
                    else:
                        pg.stt("dve", hid[:, j, :], pc, vcol(V_CFB, oc), fg[:, :], ALU.add, ALU.mult)
        for u in range(8):
            sl = w4(wget("wdown"), 1, 22)
            pq = bank()
            for kt in range(22):
                pg.mm(pq, sl[:, 0, kt, :], hid[:, kt, :], start=(kt == 0), stop=(kt == 21))
            pg.stt("dve", xb[:, u, :], pq, modv[:, 40 + u:41 + u], xb[:, u, :], ALU.mult, ALU.add)
        rstd = rms_stats(xb)
        for ft in range(8):
            o = ost[ft % 2]
            pg.stt("dve", o[:, :], xb[:, ft, :], vcol(V_NFG, ft), rstd, ALU.mult, ALU.mult)
            pg.dma("sp", outT_v[:, ft, t0:t0 + TB], o[:, :], "o%d" % (ft % 2))

    pg.emit(final_waits=["o0", "o1"])
    return nc, pg


_CACHE = {}


def kernel(**inputs):
    from concourse.bass_utils import run_bass_kernel_spmd
    sh, per = host_prep(inputs)
    S = per[0]["xT"].shape[1]
    nc, pg = build(S)
    in_maps = []
    for p in per:
        m = dict(sh)
        m.update(p)
        in_maps.append(m)
    n = len(in_maps)
    res = run_bass_kernel_spmd(nc, in_maps, core_ids=list(range(n)))
    out = np.stack([np.ascontiguousarray(r["outT"].T) for r in res.results], axis=0)
    return out.astype(np.float32)
```

```python
import numpy as np
import concourse.bass as bass
import concourse.mybir as mybir

F32 = mybir.dt.float32
BF16 = mybir.dt.bfloat16
AF = mybir.ActivationFunctionType
ALU = mybir.AluOpType
AX = mybir.AxisListType

_ESZ = {F32: 4, BF16: 2}

SB_LO = 16512
SB_HI = 229344


class View:
    __slots__ = ("ap", "space", "base", "reg")

    def __init__(self, ap, space, base):
        self.ap = ap
        self.space = space
        self.base = base
        pat = ap.ap
        S = pat[0][0]
        off = ap.offset
        esz = _ESZ[ap.dtype]
        if S == 0:
            p0, lo = 0, off
        else:
            p0, lo = off // S, off % S
        hi = lo + 1
        for st, cnt in pat[1:]:
            hi += (cnt - 1) * st
        b0 = base + lo * esz
        b1 = base + hi * esz
        if space == "P":
            b0 = (b0 // 2048) * 2048
            b1 = ((b1 + 2047) // 2048) * 2048
            self.reg = ("P", 0, 128, b0, b1)
        else:
            self.reg = ("S", p0, p0 + pat[0][1], b0, b1)

    def __getitem__(self, key):
        return View(self.ap[key], self.space, self.base)

    def f(self, fn):
        return View(fn(self.ap), self.space, self.base)

    def bc(self, axis, n):
        a = self.ap.unsqueeze(axis)
        shp = list(a.shape)
        shp[axis] = n
        return View(a.to_broadcast(shp), self.space, self.base)

    def bitcast(self, dt):
        return View(self.ap.bitcast(dt), self.space, self.base)

    @property
    def shape(self):
        return self.ap.shape


class Buf:
    def __init__(self, prog, name, shape, dtype, space="S"):
        self.name = name
        self.shape = list(shape)
        self.dtype = dtype
        self.space = space
        nc = prog.nc
        free = int(np.prod(shape[1:]))
        nbytes = free * _ESZ[dtype]
        if space == "S":
            off = prog.sb_alloc(nbytes)
            self.base = off
            self.t = nc.alloc_sbuf_tensor_at(name, list(shape), dtype, offset=off)
        else:
            self.base = 0
            self.t = nc.alloc_psum_tensor(name, list(shape), dtype)
        self.nbytes = nbytes

    def __getitem__(self, key):
        return View(self.t[key], self.space, self.base)

    def alias(self, prog, name, shape, dtype, byte_off=0):
        b = Buf.__new__(Buf)
        b.name = name
        b.shape = list(shape)
        b.dtype = dtype
        b.space = "S"
        b.base = self.base + byte_off
        b.nbytes = int(np.prod(shape[1:])) * _ESZ[dtype]
        assert byte_off + b.nbytes <= self.nbytes, (name, byte_off, b.nbytes, self.nbytes)
        b.t = prog.nc.alloc_sbuf_tensor_at(name, list(shape), dtype, offset=b.base)
        return b


class Op:
    __slots__ = ("q", "fn", "deps", "dma_sem", "dma_val", "signal", "count", "idx")


QUEUES = ("pe", "act", "dve", "pool", "sp")


class Prog:
    def __init__(self, nc):
        self.nc = nc
        self.ops = []
        self.sb_ptr = SB_LO
        self.live = {}
        self.dma_counts = {}
        self.dma_sems = {}
        self.n_dma_sems = 0

    def sb_alloc(self, nbytes, align=64):
        off = (self.sb_ptr + align - 1) // align * align
        assert off + nbytes <= SB_HI, ("SBUF overflow", off, nbytes)
        self.sb_ptr = off + nbytes
        return off

    def sbuf(self, name, shape, dtype):
        return Buf(self, name, shape, dtype, "S")

    def psum(self, name, shape, dtype=F32):
        return Buf(self, name, shape, dtype, "P")

    @staticmethod
    def _buckets(reg):
        sp, p0, p1, b0, b1 = reg
        if sp == "P":
            return [("P", k) for k in range(b0 // 2048, (b1 + 2047) // 2048)]
        return [("S", k) for k in range(b0 // 1024, (b1 + 1023) // 1024)]

    @staticmethod
    def _overlap(a, b):
        return a[0] == b[0] and a[1] < b[2] and b[1] < a[2] and a[3] < b[4] and b[3] < a[4]

    @staticmethod
    def _covers(a, b):
        return a[1] <= b[1] and a[2] >= b[2] and a[3] <= b[3] and a[4] >= b[4]

    def _track(self, idx, q, reads, writes):
        deps = {}
        for r in reads:
            for bk in self._buckets(r):
                for rec in self.live.get(bk, ()):
                    if rec[2] and self._overlap(rec[0], r):
                        deps[rec[1]] = True
        for w in writes:
            for bk in self._buckets(w):
                for rec in self.live.get(bk, ()):
                    if self._overlap(rec[0], w):
                        if rec[1] != idx and rec[1] not in deps:
                            deps[rec[1]] = False
        for w in writes:
            for bk in self._buckets(w):
                lst = self.live.setdefault(bk, [])
                lst[:] = [rec for rec in lst if not self._covers(w, rec[0])]
                lst.append([w, idx, True, q])
        for r in reads:
            for bk in self._buckets(r):
                lst = self.live.setdefault(bk, [])
                lst[:] = [rec for rec in lst
                          if not ((not rec[2]) and rec[3] == q and self._covers(r, rec[0]))]
                lst.append([r, idx, False, q])
        deps.pop(idx, None)
        return deps

    def add(self, q, fn, reads=(), writes=(), dma_sem=None):
        op = Op()
        op.q = q
        op.fn = fn
        op.idx = len(self.ops)
        rr = [v.reg for v in reads if isinstance(v, View)]
        ww = [v.reg for v in writes if isinstance(v, View)]
        op.deps = self._track(op.idx, q, rr, ww)
        op.dma_sem = dma_sem
        op.dma_val = None
        if dma_sem is not None:
            self.dma_counts[dma_sem] = self.dma_counts.get(dma_sem, 0) + 16
            op.dma_val = self.dma_counts[dma_sem]
        op.signal = False
        op.count = None
        self.ops.append(op)
        return op

    def _eng(self, e, q):
        return e

    def mm(self, out, lhsT, rhs, start=True, stop=True, **kw):
        return self.add("pe", lambda e: e.matmul(out.ap, lhsT.ap, rhs.ap, start=start, stop=stop, **kw),
                        reads=[lhsT, rhs], writes=[out])

    def transpose(self, out, in_, ident):
        return self.add("pe", lambda e: e.transpose(out.ap, in_.ap, ident.ap),
                        reads=[in_, ident], writes=[out])

    def act(self, out, in_, func, bias=None, scale=None, accum_out=None, q="act"):
        kw = {}
        rd = [in_]
        wr = [out]
        if bias is not None:
            kw["bias"] = bias.ap if isinstance(bias, View) else bias
            rd.append(bias)
        if scale is not None:
            kw["scale"] = scale.ap if isinstance(scale, View) else scale
            rd.append(scale)
        if accum_out is not None:
            kw["accum_out"] = accum_out.ap
            wr.append(accum_out)
        return self.add(q, lambda e: e.activation(out.ap, in_.ap, func, **kw), reads=rd, writes=wr)

    def tt(self, q, out, in0, in1, op):
        return self.add(q, lambda e: e.tensor_tensor(out.ap, in0.ap, in1.ap, op),
                        reads=[in0, in1], writes=[out])

    def ts(self, q, out, in0, s1, op0, s2=None, op1=None, accum_out=None):
        rd = [in0, s1, s2]
        a1 = s1.ap if isinstance(s1, View) else s1
        a2 = s2.ap if isinstance(s2, View) else s2
        kw = {}
        wr = [out]
        if op1 is not None:
            kw["op1"] = op1
        if accum_out is not None:
            kw["accum_out"] = accum_out.ap
            wr.append(accum_out)
        return self.add(q, lambda e: e.tensor_scalar(out.ap, in0.ap, a1, a2, op0, **kw),
                        reads=rd, writes=wr)

    def stt(self, q, out, in0, scalar, in1, op0, op1):
        a = scalar.ap if isinstance(scalar, View) else scalar
        return self.add(q, lambda e: e.scalar_tensor_tensor(out.ap, in0.ap, a, in1.ap, op0, op1),
                        reads=[in0, scalar, in1], writes=[out])

    def copy(self, q, out, in_):
        if q == "act":
            return self.add(q, lambda e: e.copy(out.ap, in_.ap), reads=[in_], writes=[out])
        return self.add(q, lambda e: e.tensor_copy(out.ap, in_.ap), reads=[in_], writes=[out])

    def memset(self, q, out, val):
        return self.add(q, lambda e: e.memset(out.ap, val), reads=[], writes=[out])

    def dma(self, q, out, in_, sem):
        o = out.ap if isinstance(out, View) else out
        i = in_.ap if isinstance(in_, View) else in_
        if sem not in self.dma_sems:
            self.dma_sems[sem] = None
        return self.add(q, lambda e: e.dma_start(out=o, in_=i),
                        reads=[in_] if isinstance(in_, View) else [],
                        writes=[out] if isinstance(out, View) else [],
                        dma_sem=sem)

    def emit(self, final_waits=()):
        nc = self.nc
        ops = self.ops
        need = []
        for op in ops:
            best = {}
            for j, is_raw in op.deps.items():
                pj = ops[j]
                if pj.dma_sem is not None:
                    key = ("d", pj.dma_sem)
                    if key not in best or best[key].idx < pj.idx:
                        best[key] = pj
                    continue
                if pj.q == op.q:
                    if op.q == "pe":
                        continue
                key = ("c", pj.q)
                if key not in best or best[key].idx < pj.idx:
                    best[key] = pj
            for key, pj in best.items():
                if key[0] == "c":
                    pj.signal = True
            need.append(best)
        cnt = {q: 0 for q in QUEUES}
        for op in ops:
            if op.signal:
                cnt[op.q] += 1
                op.count = cnt[op.q]
        self.sig_counts = dict(cnt)
        from contextlib import ExitStack
        with ExitStack() as es:
            csem = {q: es.enter_context(nc.semaphore("c_" + q)) for q in QUEUES}
            dsem = {s: es.enter_context(nc.semaphore("d_" + s)) for s in self.dma_sems}
            block = es.enter_context(nc.Block())
            per_q = {q: [] for q in QUEUES}
            for op, nd in zip(ops, need):
                per_q[op.q].append((op, nd))

            def run_queue(q, eng):
                waited = {}
                for op, nd in per_q[q]:
                    for key, pj in nd.items():
                        if key[0] == "d":
                            sem, val = dsem[key[1]], pj.dma_val
                        else:
                            sem, val = csem[key[1]], pj.count
                        if waited.get(key, 0) >= val:
                            continue
                        waited[key] = val
                        eng.wait_ge(sem, val)
                    ins = op.fn(eng)
                    if op.dma_sem is not None:
                        ins.then_inc(dsem[op.dma_sem], 16)
                    elif op.signal:
                        ins.then_inc(csem[q], 1)
                if q == "sp":
                    for s in final_waits:
                        eng.wait_ge(dsem[s], self.dma_counts[s])

            @block.tensor
            def _(e):
                run_queue("pe", e)

            @block.scalar
            def _(e):
                run_queue("act", e)

            @block.vector
            def _(e):
                run_queue("dve", e)

            @block.gpsimd
            def _(e):
                run_queue("pool", e)

            @block.sync
            def _(e):
                run_queue("sp", e)


D = 1024
S_FULL = 2048
TB = 512
KB = 1024
EPS = 1e-6
EPS_SSM = 1e-5

V_CT, V_BADA, V_N1G, V_CAB, V_LNG, V_LNB, V_BAO, V_BG, V_N2G, V_CFB, V_NFG, V_CBCB = \
    0, 8, 56, 64, 72, 80, 88, 96, 112, 120, 164, 172
V_CAW = 180
V_CSW = V_CAW + 248
V_CFW = V_CSW + 96
V_CSB = V_CFW + 132
V_CFW2 = V_CSB + 24
NV = V_CFW2 + 132

C_ID, C_L2, C_TRT, C_BON, C_OTOP, C_OBOT, C_TRI2 = 0, 128, 256, 384, 512, 640, 768
NCST = 768 + 64


def _units_fm(W, starts, G, KT):
    nu = len(starts) // G
    out = np.empty((nu, 128, G, KT, 128), np.float32)
    for i, s in enumerate(starts):
        blk = W[:, s:s + 128].reshape(KT, 128, 128)
        out[i // G, :, i % G] = blk.transpose(1, 0, 2)
    return out.reshape(nu, 128, G * KT * 128)


def _fm_vec(v):
    return np.ascontiguousarray(v.reshape(-1, 128).T)


def host_prep(inp):
    f = lambda k: np.asarray(inp[k], np.float32)
    w_in = f("w_in")[0]
    sh = {}
    wada = f("w_ada")[0]
    sh["wada"] = _units_fm(wada, [i * 128 for i in range(48)], 4, 8)
    ag_starts = []
    for c in range(8):
        ag_starts += [1024 + c * 128, c * 128]
    sh["wag"] = _units_fm(w_in, ag_starts, 4, 8)
    sh["wxs"] = _units_fm(w_in, [4096 + i * 128 for i in range(16)], 4, 8)
    sh["wbc"] = _units_fm(w_in, [6144 + i * 128 for i in range(8)], 4, 8)
    wz = w_in[:, 2048:4096].reshape(8, 128, 4, 512)
    sh["wz"] = np.ascontiguousarray(wz.transpose(2, 1, 0, 3)).reshape(4, 128, 4096)
    sh["wdt"] = np.ascontiguousarray(w_in[:, 7168:7200].reshape(8, 128, 32).transpose(1, 0, 2)).reshape(128, 256)
    sh["wgate"] = _units_fm(f("w_gate")[0], [i * 128 for i in range(16)], 4, 8)
    sh["waout"] = _units_fm(f("w_a_out")[0], [i * 128 for i in range(8)], 4, 8)
    sh["wbout"] = _units_fm(f("w_b_out")[0], [i * 128 for i in range(8)], 2, 16)
    sh["wo"] = _units_fm(f("w_o")[0], [i * 128 for i in range(8)], 4, 8)
    up_starts = []
    for j in range(22):
        up_starts += [j * 128, 2816 + j * 128]
    sh["wup"] = _units_fm(f("w_up")[0], up_starts, 4, 8)
    sh["wdown"] = _units_fm(f("w_down")[0], [i * 128 for i in range(8)], 1, 22)
    vec = np.zeros((128, NV), np.float32)
    vec[:, V_BADA:V_BADA + 48] = _fm_vec(f("b_ada")[0])
    vec[:, V_N1G:V_N1G + 8] = _fm_vec(f("norm1_g")[0])
    vec[:, V_CAB:V_CAB + 8] = _fm_vec(f("conv_a_b")[0])
    vec[:, V_LNG:V_LNG + 8] = _fm_vec(f("ln_a_g")[0])
    vec[:, V_LNB:V_LNB + 8] = _fm_vec(f("ln_a_b")[0])
    vec[:, V_BAO:V_BAO + 8] = _fm_vec(f("b_a_out")[0])
    vec[:, V_BG:V_BG + 16] = _fm_vec(f("b_gate")[0])
    vec[:, V_N2G:V_N2G + 8] = _fm_vec(f("norm2_g")[0])
    vec[:, V_CFB:V_CFB + 44] = _fm_vec(f("conv_ffn_b")[0])
    vec[:, V_NFG:V_NFG + 8] = _fm_vec(f("norm_f_g"))
    csb = f("conv_ssm_b")[0]
    vec[:, V_CSB:V_CSB + 24] = _fm_vec(csb)
    caw = f("conv_a_w")[0]
    vec[:, V_CAW:V_CAW + 248] = caw.T.reshape(8, 128, 31).transpose(1, 0, 2).reshape(128, 248)
    csw = f("conv_ssm_w")[0]
    vec[:, V_CSW:V_CSW + 96] = csw.T.reshape(24, 128, 4).transpose(1, 0, 2).reshape(128, 96)
    cfw = f("conv_ffn_w")[0]
    cfw_l = cfw.T.reshape(44, 128, 3).transpose(1, 0, 2)
    vec[:, V_CFW:V_CFW + 132] = cfw_l.reshape(128, 132)
    order = []
    for j in range(22):
        order += [j, 22 + j]
    vec[:, V_CFW2:V_CFW2 + 132] = cfw_l[:, order, :].reshape(128, 132)
    tmc = np.zeros((128, 96), np.float32)
    tmc[:, 0:32] = f("dt_bias")[0][None, :]
    tmc[:, 32:64] = f("a_log")[0][None, :]
    tmc[:, 64:96] = f("d_skip")[0][None, :]
    sh["tmc"] = tmc
    sh["gnb"] = np.ascontiguousarray(np.broadcast_to(f("ssm_norm_g")[0][None, :], (128, 2048)))
    cst = np.zeros((128, NCST), np.float32)
    idx = np.arange(128)
    same = (idx[:, None] // 64) == (idx[None, :] // 64)
    cst[:, C_ID:C_ID + 128] = np.eye(128)
    cst[:, C_L2:C_L2 + 128] = same & (idx[:, None] > idx[None, :])
    cst[:, C_TRT:C_TRT + 128] = same & (idx[:, None] <= idx[None, :])
    cst[:, C_BON:C_BON + 128] = same
    cst[:, C_OTOP:C_OTOP + 128] = (idx[:, None] < 64)
    cst[:, C_OBOT:C_OBOT + 128] = (idx[:, None] >= 64)
    cst[:, C_TRI2:C_TRI2 + 64] = ((idx[:, None] % 64) <= np.arange(64)[None, :])
    sh["cst"] = cst
    x = f("x")
    c = f("c")
    per = []
    for b in range(x.shape[0]):
        v = vec.copy()
        v[:, V_CT:V_CT + 8] = _fm_vec(c[b])
        per.append({"xT": np.ascontiguousarray(x[b].T), "vec": v})
    return sh, per


def build(S):
    NBLK = S // TB
    nc = bass.Bass("TRN2", target_bir_lowering=False)
    pg = Prog(nc)

    def din(name, shape):
        return nc.dram_tensor(name, list(shape), F32, kind="ExternalInput").ap()

    xT = din("xT", [D, S])
    vec_d = din("vec", [128, NV])
    cst_d = din("cst", [128, NCST])
    tmc_d = din("tmc", [128, 96])
    gnb_d = din("gnb", [128, 2048])
    wdt_d = din("wdt", [128, 256])
    wd = {k: din(k, shp) for k, shp in [
        ("wada", [12, 128, 4096]), ("wag", [4, 128, 4096]), ("wxs", [4, 128, 4096]),
        ("wbc", [2, 128, 4096]), ("wz", [4, 128, 4096]), ("wgate", [4, 128, 4096]),
        ("waout", [2, 128, 4096]), ("wbout", [4, 128, 4096]), ("wo", [2, 128, 4096]),
        ("wup", [11, 128, 4096]), ("wdown", [8, 128, 2816])]}
    outT = nc.dram_tensor("outT", [D, S], F32, kind="ExternalOutput").ap()

    base = SB_LO

    def at(name, shape, dtype, off):
        b = Buf.__new__(Buf)
        b.name, b.shape, b.dtype, b.space = name, list(shape), dtype, "S"
        b.base = base + off
        b.nbytes = int(np.prod(shape[1:])) * _ESZ[dtype]
        assert b.base % 32 == 0 and b.base + b.nbytes <= SB_HI, (name, off, b.nbytes)
        b.t = nc.alloc_sbuf_tensor_at(name, list(shape), dtype, offset=b.base)
        b.end = off + b.nbytes
        return b

    class Seq:
        def __init__(self, start, limit):
            self.p, self.limit = start, limit

        def __call__(self, name, shape, dtype):
            off = (self.p + 63) // 64 * 64
            b = at(name, shape, dtype, off)
            self.p = b.end
            assert self.p <= self.limit, ("region overflow", name, self.p, self.limit)
            return b

    P = Seq(0, 68 * KB)
    cst = P("cst", [128, NCST], F32)
    vec = P("vec", [128, NV], F32)
    tmc = P("tmc", [128, 96], F32)
    vecb = P("vecb", [128, NV], BF16)
    idb = P("idb", [128, 128], BF16)
    onb = P("onb", [128, 128], BF16)
    onf = P("onf", [128, 128], F32)
    modv = P("modv", [128, 48], F32)
    gs = P("gs", [128, 16], F32)
    abc = P("abc", [128, 32], F32)
    dbc = P("dbc", [128, 32], BF16)
    scb = P("scb", [128, 8], BF16)
    wdt = P("wdt", [128, 8, 32], BF16)
    gnb = P("gnb", [128, 2048], BF16)
    halo_u = P("halo_u", [128, 8, 30], BF16)
    halo_x = P("halo_x", [128, 24, 3], BF16)
    halo_f = P("halo_f", [128, 44, 2], BF16)
    Sst = P("Sst", [128, 4, 512], F32)
    SbA = [P("SbA0", [128, 4, 512], BF16), P("SbA1", [128, 4, 512], BF16)]
    SbB = P("SbB", [128, 4, 512], BF16)
    NSLOT = 3
    wslots = [P("wsl%d" % i, [128, 4096], BF16) for i in range(NSLOT)]
    hb = P("hb", [128, 8, 512], BF16)
    assert P.p <= 68 * KB, P.p
    M = Seq(68 * KB, 144 * KB)
    sz = M("sz", [128, 4, 2048], BF16)
    xsT = M("xsT", [128, 4, 2048], BF16)
    BT = M("BT", [128, 4, 512], BF16)
    BfT = M("BfT", [128, 4, 512], BF16)
    CfT = M("CfT", [128, 4, 512], BF16)
    C0 = M("C0", [128, 4, 4, 128], BF16)
    C1 = M("C1", [128, 4, 4, 128], BF16)
    ua = M("ua", [128, 8, 512], F32)
    ynT = at("ynT", [128, 16, 512], BF16, ua.base - base)
    uA = M("uA", [128, 8, 512], BF16)
    hid = at("hid", [128, 22, 512], BF16, 68 * KB)
    A0 = 144 * KB
    A_END = SB_HI - base
    xb = at("xb", [128, 8, 512], F32, A_END - 16 * KB - 64)
    A_LIM = A_END - 16 * KB - 64
    G_ = Seq(A0, A_LIM)
    sq = G_("sq", [128, 8, 512], BF16)
    st0 = G_("st0", [128, 512], F32)
    st1 = G_("st1", [128, 512], F32)
    st2 = G_("st2", [128, 512], F32)
    tmpf = [G_("tmpf0", [128, 512], F32), G_("tmpf1", [128, 512], F32)]
    DB = 16
    dring = [G_("dgr%d" % i, [128, DB, 128], BF16) for i in range(2)]
    g_end = G_.p
    Q = Seq(g_end, A_END)
    ub = Q("ub", [128, 8, 542], BF16)
    xbcp = Q("xbcp", [128, 24, 515], BF16)
    sgs = [Q("sg0", [128, 512], BF16), Q("sg1", [128, 512], BF16)]
    gab = at("gab", [128, 16, 512], BF16, sz.base - base)
    t1 = at("t1", [128, 8, 512], BF16, xsT.base - base)
    Q3 = Seq(g_end, A_LIM)
    pgs = [Q3("pgs%d" % i, [128, 514], BF16) for i in range(4)]
    fgs = [Q3("fg%d" % i, [128, 512], BF16) for i in range(2)]
    ost = [Q3("ost%d" % i, [128, 512], F32) for i in range(2)]
    R = Seq(A0, A_END)
    R2 = R("R2", [128, 32, 64], F32)
    eb = R("eb", [128, 32, 64], BF16)
    m2 = R("m2", [128, 32, 128], BF16)
    xdt = R("xdt", [128, 32, 64], BF16)
    xw = R("xw", [128, 32, 64], BF16)
    xsD = R("xsD", [128, 32, 64], BF16)
    tyo = R("tyo", [128, 4, 512], F32)
    sqf = R("sqf", [128, 4, 512], F32)
    yn = R("yn", [128, 2048], BF16)
    CBm = R("CBm", [128, 4, 128], BF16)
    CBc = R("CBc", [128, 4, 64], BF16)
    sm = R("sm", [128, 16, 32], F32)
    smp = at("smp", [128, 4, 6, 32], F32, A0 + 62208)

    ps = pg.psum("ps", [128, 4096], F32)
    bank_ctr = [0]

    def bank():
        i = bank_ctr[0] % 3
        bank_ctr[0] += 1
        return ps[:, 512 * i:512 * (i + 1)]

    def fixed_bank(i):
        return ps[:, 512 * i:512 * (i + 1)]

    ident_f = cst[:, C_ID:C_ID + 128]
    L2 = cst[:, C_L2:C_L2 + 128]
    trT = cst[:, C_TRT:C_TRT + 128]
    bones = cst[:, C_BON:C_BON + 128]
    otop = cst[:, C_OTOP:C_OTOP + 128]
    obot = cst[:, C_OBOT:C_OBOT + 128]
    tri2 = cst[:, C_TRI2:C_TRI2 + 64]

    def vcol(c0, i=0):
        return vec[:, c0 + i:c0 + i + 1]

    blk1 = ([("wag", u) for u in range(4)] + [("wxs", u) for u in range(4)] +
            [("wbc", u) for u in range(2)] + [("wz", u) for u in range(4)])
    blk2 = ([("wgate", u) for u in range(4)] + [("waout", u) for u in range(2)] +
            [("wbout", u) for u in range(4)] + [("wo", u) for u in range(2)] +
            [("wup", u) for u in range(11)] + [("wdown", u) for u in range(8)])
    sched = [("wada", u) for u in range(4)]
    for b_ in range(NBLK):
        sched += blk1
        if b_ == 0:
            sched += [("wada", u) for u in range(4, 12)]
        sched += blk2
    wstate = {"issued": 0, "used": 0}

    def w_issue_upto(n):
        while wstate["issued"] < min(n, len(sched)):
            i = wstate["issued"]
            name, u = sched[i]
            ncol = 2816 if name == "wdown" else 4096
            slot = wslots[i % NSLOT]
            pg.dma("pool", slot[:, 0:ncol], wd[name][u], "w%d" % (i % NSLOT))
            wstate["issued"] += 1

    def wget(name):
        i = wstate["used"]
        assert sched[i][0] == name, (sched[i], name)
        w_issue_upto(i + NSLOT)
        wstate["used"] += 1
        return wslots[i % NSLOT]

    def w4(slot, G, KT):
        return slot[:, 0:G * KT * 128].f(lambda a: a.rearrange("p (g k m) -> p g k m", g=G, k=KT))

    pg.dma("sp", cst[:, :], cst_d, "c0")
    pg.dma("sp", vec[:, :], vec_d, "c1")
    pg.dma("sp", tmc[:, :], tmc_d, "c2")
    pg.dma("pool", gnb[:, :], gnb_d, "c3")
    pg.dma("pool", wdt[:, :, :].f(lambda a: a.rearrange("p k n -> p (k n)")), wdt_d, "c4")
    w_issue_upto(NSLOT)
    pg.copy("dve", idb[:, :], ident_f)
    pg.copy("dve", vecb[:, :], vec[:, :])
    pg.memset("dve", onb[:, :], 1.0)
    pg.memset("dve", onf[:, :], 1.0)
    pg.memset("pool", halo_u[:, :, :], 0.0)
    pg.memset("pool", halo_x[:, :, :], 0.0)
    pg.memset("pool", halo_f[:, :, :], 0.0)
    pg.memset("pool", Sst[:, :, :], 0.0)
    pg.memset("pool", SbA[0][:, :, :], 0.0)
    pg.memset("pool", C0[:, :, :, :], 0.0)
    pg.memset("pool", C1[:, :, :, :], 0.0)
    pg.act(abc[:, :], tmc[:, 32:64], AF.Exp)
    pg.ts("dve", abc[:, :], abc[:, :], -1.0, ALU.mult)
    pg.copy("dve", dbc[:, :], tmc[:, 64:96])
    pg.act(scb[:, :], vec[:, V_CT:V_CT + 8], AF.Silu)
    def mod_part(u0, u1):
        psm = fixed_bank(3)
        for u in range(u0, u1):
            sl = w4(wget("wada"), 4, 8)
            for g in range(4):
                oc = 4 * u + g
                for kt in range(8):
                    pg.mm(psm[:, oc:oc + 1], sl[:, g, kt, :], scb[:, kt:kt + 1], start=(kt == 0), stop=(kt == 7))
        c0, c1 = 4 * u0, 4 * u1
        pg.tt("dve", modv[:, c0:c1], psm[:, c0:c1], vec[:, V_BADA + c0:V_BADA + c1], ALU.add)

    mod_part(0, 4)
    pg.stt("dve", gs[:, 0:8], modv[:, 8:16], 1.0, vec[:, V_N1G:V_N1G + 8], ALU.add, ALU.mult)

    evac_ctr = [0]

    def evac_copy(out, in_):
        q = "act" if evac_ctr[0] % 2 == 0 else "dve"
        evac_ctr[0] += 1
        pg.copy(q, out, in_)

    dg_ctr = [0]

    def diag_batch(c0, n):
        r = dring[dg_ctr[0] % 2]
        dg_ctr[0] += 1
        pg.tt("dve", r[:, 0:n, :], idb[:, :].bc(1, n), vecb[:, c0:c0 + n].bc(2, 128), ALU.mult)
        return [r[:, i, :] for i in range(n)]

    def rms_stats(xbuf):
        pst = fixed_bank(3)
        for ft in range(8):
            pg.act(sq[:, ft, :], xbuf[:, ft, :], AF.Square)
        for ft in range(8):
            pg.mm(pst, onb[:, :], sq[:, ft, :], start=(ft == 0), stop=(ft == 7))
        pg.act(st0[:, :], pst, AF.Sqrt, bias=EPS, scale=1.0 / D)
        pg.add("dve", lambda e: e.reciprocal(st1[:, :].ap, st0[:, :].ap), reads=[st0[:, :]], writes=[st1[:, :]])
        return st1[:, :]

    def mod_norm(xbuf, gcol0, shcol0, out):
        rstd = rms_stats(xbuf)
        for ft in range(8):
            t = tmpf[ft % 2]
            pg.tt("dve", t[:, :], xbuf[:, ft, :], rstd, ALU.mult)
            pg.act(out[:, ft, :], t[:, :], AF.Identity, bias=modv[:, shcol0 + ft:shcol0 + ft + 1],
                   scale=gs[:, gcol0 + ft:gcol0 + ft + 1])

    xT_v = xT.rearrange("(f p) s -> p f s", p=128)
    outT_v = outT.rearrange("(f p) s -> p f s", p=128)

    for blk in range(NBLK):
        t0 = blk * TB
        pg.dma("sp", xb[:, :, :], xT_v[:, :, t0:t0 + TB], "x")
        mod_norm(xb, 0, 0, hb)
        pg.copy("pool", ub[:, :, 0:30], halo_u[:, :, :])
        pg.copy("pool", xbcp[:, :, 0:3], halo_x[:, :, :])
        for u in range(4):
            sl = w4(wget("wag"), 4, 8)
            for half in range(2):
                c = 2 * u + half
                pa = bank()
                for kt in range(8):
                    pg.mm(pa, sl[:, 2 * half, kt, :], hb[:, kt, :], start=(kt == 0), stop=(kt == 7))
                sg = sgs[c % 2]
                pg.act(sg[:, :], pa, AF.Sigmoid)
                pv = bank()
                for kt in range(8):
                    pg.mm(pv, sl[:, 2 * half + 1, kt, :], hb[:, kt, :], start=(kt == 0), stop=(kt == 7))
                pg.tt("dve", ub[:, c, 30:542], pv, sg[:, :], ALU.mult)
        for (nm, nu, ct0) in (("wxs", 4, 0), ("wbc", 2, 16)):
            for u in range(nu):
                sl = w4(wget(nm), 4, 8)
                for g in range(4):
                    ct = ct0 + 4 * u + g
                    pp = bank()
                    for kt in range(8):
                        pg.mm(pp, sl[:, g, kt, :], hb[:, kt, :], start=(kt == 0), stop=(kt == 7))
                    evac_copy(xbcp[:, ct, 3:515], pp)
        for u in range(4):
            sl = wget("wz")[:, :].f(lambda a: a.rearrange("p (k n) -> p k n", k=8))
            for pr in range(4):
                pz = bank()
                for kt in range(8):
                    pg.mm(pz, hb[:, kt, pr * 128:(pr + 1) * 128], sl[:, kt, :], start=(kt == 0), stop=(kt == 7))
                pg.act(sz[:, pr, u * 512:(u + 1) * 512], pz, AF.Silu)
        for pr in range(4):
            tp = pr * 128
            pm = fixed_bank(3)
            for kt in range(8):
                pg.mm(pm[:, 0:32], hb[:, kt, tp:tp + 128], wdt[:, kt, :], start=(kt == 0), stop=(kt == 7))
            s_xb, s_ex, s_A, s_d2 = [tmpf[0][:, 32 * i:32 * (i + 1)] for i in range(4)]
            s_dt, s_da, s_eA, s_w, s_cd0, s_cd1 = [smp[:, pr, i, :] for i in range(6)]
            pg.tt("dve", s_xb, pm[:, 0:32], tmc[:, 0:32], ALU.add)
            pg.act(s_ex, s_xb, AF.Exp)
            pg.act(s_dt, s_ex, AF.Ln, bias=1.0)
            pg.tt("dve", s_da, s_dt, abc[:, :], ALU.mult)
            pg.mm(pm[:, 32:64], trT, s_da)
            pg.mm(pm[:, 64:96], bones, s_da)
            pg.mm(pm[:, 96:128], otop, s_da)
            pg.mm(pm[:, 128:160], obot, s_da)
            pg.act(s_eA, pm[:, 32:64], AF.Exp)
            pg.copy("act", s_A, pm[:, 32:64])
            pg.tt("dve", s_d2, pm[:, 64:96], s_A, ALU.subtract)
            pg.act(s_d2, s_d2, AF.Exp)
            pg.tt("dve", s_w, s_dt, s_d2, ALU.mult)
            pg.act(s_cd0, pm[:, 96:128], AF.Exp)
            pg.act(s_cd1, pm[:, 128:160], AF.Exp)
        for cg in range(6):
            dw = diag_batch(V_CSW + cg * 16, 16)
            dlist = [[dw[4 * j + k] for k in range(4)] for j in range(4)]
            if cg < 5:
                dbs = diag_batch(V_CSB + cg * 4, 4)
                for j in range(4):
                    dlist[j].append(dbs[j])
            if cg < 5:
                dst = xsT if cg < 4 else BT
                for pr in range(4):
                    pt = bank()
                    for j in range(4):
                        ct = 4 * cg + j
                        o = pt[:, j * 128:(j + 1) * 128]
                        for k in range(4):
                            pg.mm(o, xbcp[:, ct, pr * 128 + k:pr * 128 + k + 128], dlist[j][k],
                                  start=(k == 0), stop=False)
                        pg.mm(o, onb[:, :], dlist[j][4], start=False, stop=True)
                    if cg < 4:
                        pg.act(xsT[:, pr, cg * 512:(cg + 1) * 512], pt, AF.Silu)
                    else:
                        pg.act(BT[:, pr, :], pt, AF.Silu)
            if cg >= 4:
                dstf = BfT if cg == 4 else CfT
                for j in range(4):
                    ct = 4 * cg + j
                    pf = bank()
                    for k in range(4):
                        pg.mm(pf, dlist[j][k], xbcp[:, ct, k:k + 512], start=(k == 0), stop=(k == 3))
                    pg.act(dstf[:, j, :], pf, AF.Silu, bias=vcol(V_CSB, ct))
                    if cg == 5:
                        cv = CfT[:, j, :].f(lambda a: a.rearrange("p (r c l) -> p r c l", r=4, c=2))
                        pg.copy("pool", C0[:, j, :, 0:64], cv[:, :, 0, :])
                        pg.copy("pool", C1[:, j, :, 64:128], cv[:, :, 1, :])
        pg.copy("pool", halo_x[:, :, :], xbcp[:, :, 512:515])
        ps1 = fixed_bank(3)
        for ct in range(8):
            pc = bank()
            for k0 in (0, 16):
                nk = min(16, 31 - k0)
                dks = diag_batch(V_CAW + ct * 31 + k0, nk)
                for kk in range(nk):
                    k = k0 + kk
                    pg.mm(pc, dks[kk], ub[:, ct, k:k + 512], start=(k == 0), stop=(k == 30))
            pg.act(ua[:, ct, :], pc, AF.Identity, bias=vcol(V_CAB, ct))
            pg.act(sq[:, ct, :], pc, AF.Square, bias=vcol(V_CAB, ct))
        pg.copy("pool", halo_u[:, :, :], ub[:, :, 512:542])
        for ct in range(8):
            pg.mm(ps1, onf[:, :], ua[:, ct, :], start=(ct == 0), stop=(ct == 7))
        ps2 = bank()
        for ct in range(8):
            pg.mm(ps2, onb[:, :], sq[:, ct, :], start=(ct == 0), stop=(ct == 7))
        pg.ts("dve", st0[:, :], ps1, 1.0 / D, ALU.mult)
        pg.tt("dve", st2[:, :], st0[:, :], st0[:, :], ALU.mult)
        pg.stt("dve", st2[:, :], ps2, 1.0 / D, st2[:, :], ALU.mult, ALU.subtract)
        pg.act(st2[:, :], st2[:, :], AF.Sqrt, bias=EPS)
        pg.add("dve", lambda e: e.reciprocal(st1[:, :].ap, st2[:, :].ap), reads=[st2[:, :]], writes=[st1[:, :]])
        for ct in range(8):
            t = tmpf[ct % 2]
            pg.tt("dve", t[:, :], ua[:, ct, :], st0[:, :], ALU.subtract)
            pg.tt("dve", t[:, :], t[:, :], st1[:, :], ALU.mult)
            pg.act(uA[:, ct, :], t[:, :], AF.Silu, bias=vcol(V_LNB, ct), scale=vcol(V_LNG, ct))

        if blk == 0:
            mod_part(4, 12)
            pg.stt("dve", gs[:, 8:16], modv[:, 32:40], 1.0, vec[:, V_N2G:V_N2G + 8], ALU.add, ALU.mult)
        pg.memset("pool", m2[:, :, :], 0.0)
        for pr in range(4):
            gp = blk * 4 + pr
            tp = pr * 128
            SA_in = SbA[gp % 2]
            SA_out = SbA[(gp + 1) % 2]
            s_dt, s_da, s_eA, s_w, s_cd0, s_cd1 = [smp[:, pr, i, :] for i in range(6)]
            s_ss, s_rs = sm[:, 0, :], sm[:, 1, :]
            xs3 = xsT[:, pr, :].f(lambda a: a.rearrange("p (h d) -> p h d", h=32))
            pg.tt("dve", R2[:, :, :], tri2.bc(1, 32), s_da.bc(2, 64), ALU.mult)
            R2f = R2[:, :, :].f(lambda a: a.rearrange("p h l -> p (h l)"))
            ebf = eb[:, :, :].f(lambda a: a.rearrange("p h l -> p (h l)"))
            for q in range(4):
                pg.mm(fixed_bank(4 + q), L2, R2f[:, q * 512:(q + 1) * 512])
            for q in range(4):
                pg.act(ebf[:, q * 512:(q + 1) * 512], fixed_bank(4 + q), AF.Exp)
            pg.tt("dve", xdt[:, :, :], xs3, s_dt.bc(2, 64), ALU.mult)
            pg.tt("dve", xsD[:, :, :], xs3, dbc[:, :].bc(2, 64), ALU.mult)
            pcb = bank()
            for g in range(4):
                pg.mm(pcb[:, g * 128:(g + 1) * 128], BfT[:, g, tp:tp + 128], CfT[:, g, tp:tp + 128])
            pg.tt("dve", CBm[:, :, :], pcb.f(lambda a: a.rearrange("p (g l) -> p g l", g=4)), trT.bc(1, 4), ALU.mult)
            pg.tt("dve", CBc[:, :, :], CBm[:, :, 0:64], CBm[:, :, 64:128], ALU.add)
            for g in range(4):
                for j in range(2):
                    pg.tt("dve", m2[64 * j:64 * j + 64, g * 8:(g + 1) * 8, 64 * j:64 * j + 64],
                          eb[64 * j:64 * j + 64, g * 8:(g + 1) * 8, :],
                          CBc[64 * j:64 * j + 64, g, :].bc(1, 8), ALU.mult)
            for h in range(32):
                g = h // 8
                o = fixed_bank(4 + g)[:, (h % 8) * 64:(h % 8) * 64 + 64]
                pg.mm(o, m2[:, h, :], xdt[:, h, :], start=True, stop=False)
                pg.mm(o, idb[:, :], xsD[:, h, :], start=False, stop=True)
            pg.tt("dve", xw[:, :, :], xs3, s_w.bc(2, 64), ALU.mult)
            for j in range(2):
                cdj = s_cd0 if j == 0 else s_cd1
                Sb_out = SbB if j == 0 else SA_out
                for g in range(4):
                    pd = bank()
                    pg.mm(pd, BT[64 * j:64 * j + 64, pr, g * 128:(g + 1) * 128],
                          xw[64 * j:64 * j + 64, g * 8:(g + 1) * 8, :].f(lambda a: a.rearrange("p h d -> p (h d)")))
                    S3 = Sst[:, g, :].f(lambda a: a.rearrange("p (h d) -> p h d", h=8))
                    pg.tt("pool", S3, S3, cdj[:, g * 8:(g + 1) * 8].bc(2, 64), ALU.mult)
                    pg.tt("dve", Sst[:, g, :], Sst[:, g, :], pd, ALU.add)
                    pg.copy("act", Sb_out[:, g, :], Sst[:, g, :])
            for g in range(4):
                po = bank()
                pg.mm(po, C0[:, g, pr, :], SA_in[:, g, :], start=True, stop=False)
                pg.mm(po, C1[:, g, pr, :], SbB[:, g, :], start=False, stop=True)
                ty3 = tyo[:, g, :].f(lambda a: a.rearrange("p (h d) -> p h d", h=8))
                pg.tt("dve", ty3, po.f(lambda a: a.rearrange("p (h d) -> p h d", h=8)),
                      s_eA[:, g * 8:(g + 1) * 8].bc(2, 64), ALU.mult)
                pg.tt("dve", tyo[:, g, :], tyo[:, g, :], fixed_bank(4 + g), ALU.add)
                pg.tt("dve", tyo[:, g, :], tyo[:, g, :], sz[:, pr, g * 512:(g + 1) * 512], ALU.mult)
                pg.act(sqf[:, g, :], tyo[:, g, :], AF.Square)
            pg.add("dve", lambda e: e.tensor_reduce(s_ss[:, 0:4].ap, sqf[:, :, :].ap, AX.X, ALU.add),
                   reads=[sqf[:, :, :]], writes=[s_ss[:, 0:4]])
            pg.act(s_ss[:, 0:4], s_ss[:, 0:4], AF.Sqrt, bias=EPS_SSM, scale=1.0 / 512)
            pg.add("dve", lambda e: e.reciprocal(s_rs[:, 0:4].ap, s_ss[:, 0:4].ap),
                   reads=[s_ss[:, 0:4]], writes=[s_rs[:, 0:4]])
            for g in range(4):
                pg.stt("dve", yn[:, g * 512:(g + 1) * 512], tyo[:, g, :], s_rs[:, g:g + 1],
                       gnb[:, g * 512:(g + 1) * 512], ALU.mult, ALU.mult)
            for c4 in range(4):
                ptb = bank().bitcast(BF16)
                for j in range(4):
                    ct = 4 * c4 + j
                    pg.transpose(ptb[:, j * 128:(j + 1) * 128], yn[:, ct * 128:(ct + 1) * 128], idb[:, :])
                evac_copy(ynT[:, 4 * c4:4 * c4 + 4, tp:tp + 128],
                          ptb[:, 0:512].f(lambda a: a.rearrange("p (c t) -> p c t", c=4)))

        for u in range(4):
            sl = w4(wget("wgate"), 4, 8)
            for g in range(4):
                oc = 4 * u + g
                pq = bank()
                for kt in range(8):
                    pg.mm(pq, sl[:, g, kt, :], hb[:, kt, :], start=(kt == 0), stop=(kt == 7))
                pg.act(gab[:, oc, :], pq, AF.Sigmoid, bias=vcol(V_BG, oc))
        for u in range(2):
            sl = w4(wget("waout"), 4, 8)
            for g in range(4):
                oc = 4 * u + g
                pq = bank()
                for kt in range(8):
                    pg.mm(pq, sl[:, g, kt, :], uA[:, kt, :], start=(kt == 0), stop=(kt == 7))
                pg.stt("dve", t1[:, oc, :], pq, vcol(V_BAO, oc), gab[:, oc, :], ALU.add, ALU.mult)
        for u in range(4):
            sl = w4(wget("wbout"), 2, 16)
            for g in range(2):
                oc = 2 * u + g
                pq = bank()
                for kt in range(16):
                    pg.mm(pq, sl[:, g, kt, :], ynT[:, kt, :], start=(kt == 0), stop=(kt == 15))
                t = tmpf[oc % 2]
                pg.tt("dve", t[:, :], pq, gab[:, 8 + oc, :], ALU.mult)
                pg.tt("dve", t1[:, oc, :], t1[:, oc, :], t[:, :], ALU.add)
        pg.dma("sp", xb[:, :, :], xT_v[:, :, t0:t0 + TB], "x")
        for u in range(2):
            sl = w4(wget("wo"), 4, 8)
            for g in range(4):
                oc = 4 * u + g
                pq = bank()
                for kt in range(8):
                    pg.mm(pq, sl[:, g, kt, :], t1[:, kt, :], start=(kt == 0), stop=(kt == 7))
                pg.stt("dve", xb[:, oc, :], pq, modv[:, 16 + oc:17 + oc], xb[:, oc, :], ALU.mult, ALU.add)

        mod_norm(xb, 8, 24, hb)
        pend = [None]

        def ffn_conv(item):
            sb_, dks, oc, j, gv = item
            pc = bank()
            for k in range(3):
                pg.mm(pc, dks[k], sb_[:, k:k + 512], start=(k == 0), stop=(k == 2))
            if gv == 0:
                pg.act(fgs[j % 2][:, :], pc, AF.Silu, bias=vcol(V_CFB, oc))
            else:
                pg.stt("dve", hid[:, j, :], pc, vcol(V_CFB, oc), fgs[j % 2][:, :], ALU.add, ALU.mult)

        for u in range(11):
            sl = w4(wget("wup"), 4, 8)
            dfs = diag_batch(V_CFW2 + u * 12, 12)
            for half in range(2):
                j = 2 * u + half
                for gv in range(2):
                    oc = j if gv == 0 else 22 + j
                    pq = bank()
                    for kt in range(8):
                        pg.mm(pq, sl[:, 2 * half + gv, kt, :], hb[:, kt, :], start=(kt == 0), stop=(kt == 7))
                    sb_ = pgs[(2 * j + gv) % 4]
                    pg.copy("pool", sb_[:, 0:2], halo_f[:, oc, :])
                    evac_copy(sb_[:, 2:514], pq)
                    pg.copy("pool", halo_f[:, oc, :], sb_[:, 512:514])
                    if pend[0] is not None:
                        ffn_conv(pend[0])
                    pend[0] = (sb_, [dfs[(2 * half + gv) * 3 + k] for k in range(3)], oc, j, gv)
        ffn_conv(pend[0])
        for u in range(8):
            sl = w4(wget("wdown"), 1, 22)
            pq = bank()
            for kt in range(22):
                pg.mm(pq, sl[:, 0, kt, :], hid[:, kt, :], start=(kt == 0), stop=(kt == 21))
            pg.stt("dve", xb[:, u, :], pq, modv[:, 40 + u:41 + u], xb[:, u, :], ALU.mult, ALU.add)
        rstd = rms_stats(xb)
        for ft in range(8):
            o = ost[ft % 2]
            pg.stt("dve", o[:, :], xb[:, ft, :], vcol(V_NFG, ft), rstd, ALU.mult, ALU.mult)
            pg.dma("sp", outT_v[:, ft, t0:t0 + TB], o[:, :], "o%d" % (ft % 2))

    pg.emit(final_waits=["o0", "o1"])
    return nc, pg


_CACHE = {}


def kernel(**inputs):
    from concourse.bass_utils import run_bass_kernel_spmd
    sh, per = host_prep(inputs)
    S = per[0]["xT"].shape[1]
    nc, pg = build(S)
    in_maps = []
    for p in per:
        m = dict(sh)
        m.update(p)
        in_maps.append(m)
    n = len(in_maps)
    res = run_bass_kernel_spmd(nc, in_maps, core_ids=list(range(n)))
    out = np.stack([np.ascontiguousarray(r["outT"].T) for r in res.results], axis=0)
    return out.astype(np.float32)
```
